# Optimizing a Trainium2 kernel written in Bass

```python
import math
import jax, jax.numpy as jnp
from jax import lax
import numpy as np

D_MODEL = 1024
BATCH = 4
SEQ = 4096
DEPTH = 2
DEC_BATCH = 32
DEC_SEQ = 8
PAST_LEN = 16384
PAGE_SIZE = 128

A_HEADS = 8
A_HEAD_DIM = 64
A_WIDTH = A_HEADS * A_HEAD_DIM
DILATED_PATTERNS = ((128, 1), (512, 4), (2048, 16))
WIN_MAX = max(w for w, _ in DILATED_PATTERNS)
B_HEADS = 4
B_KEY_DIM = 128
B_VAL_DIM = 128
B_KEY_WIDTH = B_HEADS * B_KEY_DIM
B_VAL_WIDTH = B_HEADS * B_VAL_DIM
HGRN_CHUNK = 64
D_FF = 2816
NORM_EPS = 1e-6
IN_SPLIT_SIZES = (A_WIDTH, A_WIDTH, A_WIDTH, B_KEY_WIDTH, B_KEY_WIDTH, B_VAL_WIDTH, B_VAL_WIDTH)
IN_WIDTH = sum(IN_SPLIT_SIZES)
MIX_WIDTH = A_WIDTH + B_VAL_WIDTH

kernel_name = 'hybrid_dilated_attn_hgrn2_macaron_step'


def rmsnorm(x, g):
    xf = x.astype(jnp.float32)
    y = xf * lax.rsqrt(jnp.mean(xf * xf, axis=-1, keepdims=True) + NORM_EPS)
    return (y * g.astype(jnp.float32)).astype(x.dtype)


def swiglu(x, w_gate, w_up, w_down):
    return (jax.nn.silu(x @ w_gate) * (x @ w_up)) @ w_down


def dilated_window_prompt(q, k, v, window, dilation):
    B, S, H, Dh = q.shape
    steps = window // dilation
    n = S // dilation
    nb = -(-n // steps)
    pad = nb * steps - n

    def to_blocks(t):
        t = t.reshape(B, n, dilation, H, Dh).transpose(0, 2, 1, 3, 4)
        t = jnp.pad(t, ((0, 0), (0, 0), (0, pad), (0, 0), (0, 0)))
        return t.reshape(B, dilation, nb, steps, H, Dh)

    def with_prev(t):
        prev = jnp.pad(t, ((0, 0), (0, 0), (1, 0), (0, 0), (0, 0), (0, 0)))[:, :, :-1]
        return jnp.concatenate([prev, t], axis=3)

    qb = to_blocks(q)
    kw = with_prev(to_blocks(k))
    vw = with_prev(to_blocks(v))
    s = jnp.einsum('brcqhd,brckhd->brchqk', qb, kw).astype(jnp.float32) / math.sqrt(Dh)
    qi = jnp.arange(steps)[:, None]
    ki = jnp.arange(2 * steps)[None, :]
    dist = steps + qi - ki
    band = (dist >= 0) & (dist <= steps)
    exists = (jnp.arange(nb) > 0)[:, None, None] | (ki >= steps)[None]
    mask = band[None] & exists
    s = jnp.where(mask[:, None], s, -jnp.inf)
    lse = jax.nn.logsumexp(s, axis=-1)
    p = jnp.exp(s - lse[..., None])
    o = jnp.einsum('brchqk,brckhd->brcqhd', p, vw.astype(jnp.float32))
    o = o.reshape(B, dilation, nb * steps, H, Dh)[:, :, :n]
    o = o.transpose(0, 2, 1, 3, 4).reshape(B, S, H, Dh)
    lse = lse.transpose(0, 1, 2, 4, 3).reshape(B, dilation, nb * steps, H)[:, :, :n]
    lse = lse.transpose(0, 2, 1, 3).reshape(B, S, H)
    return o, lse


def dilated_window_sample(q, k_all, v_all, window, dilation):
    B, T, H, Dh = q.shape
    N = k_all.shape[1]
    steps = window // dilation
    idx = (N - T + jnp.arange(T))[:, None] - dilation * jnp.arange(steps + 1)[None, :]
    valid = idx >= 0
    idx = jnp.maximum(idx, 0)
    kg = k_all[:, idx]
    vg = v_all[:, idx]
    s = jnp.einsum('bthd,btjhd->bthj', q, kg).astype(jnp.float32) / math.sqrt(Dh)
    s = jnp.where(valid[:, None, :], s, -jnp.inf)
    lse = jax.nn.logsumexp(s, axis=-1)
    p = jnp.exp(s - lse[..., None])
    o = jnp.einsum('bthj,btjhd->bthd', p, vg.astype(jnp.float32))
    return o, lse


def dilated_mixture(outs):
    o = jnp.stack([oi for oi, _ in outs])
    lse = jnp.stack([li for _, li in outs])
    w = jax.nn.softmax(lse, axis=0)
    return jnp.sum(w[..., None] * o, axis=0)


def hgrn2_scan(q, k, v, log_f, s0, chunk):
    B, T, H, K = q.shape
    V = v.shape[-1]
    nc = T // chunk

    def split(t):
        return t.reshape(B, nc, chunk, H, t.shape[-1]).transpose(1, 0, 3, 2, 4)

    causal = jnp.tril(jnp.ones((chunk, chunk), dtype=bool))

    def step(S, xs):
        qc, kc, vc, gc = xs
        G = jnp.cumsum(gc, axis=2)
        o_inter = jnp.einsum('bhck,bhkv->bhcv', qc * jnp.exp(G), S)
        diff = G[:, :, :, None, :] - G[:, :, None, :, :]
        decay = jnp.exp(jnp.where(causal[:, :, None], diff, -jnp.inf))
        A = jnp.einsum('bhtk,bhsk,bhtsk->bhts', qc, kc, decay)
        o = o_inter + jnp.einsum('bhts,bhsv->bhtv', A, vc)
        G_last = G[:, :, -1:, :]
        S_new = jnp.exp(G_last[:, :, 0, :])[..., None] * S + jnp.einsum(
            'bhck,bhcv->bhkv', kc * jnp.exp(G_last - G), vc)
        return S_new, o

    S_fin, o = lax.scan(step, s0, (split(q), split(k), split(v), split(log_f)))
    o = o.transpose(1, 0, 3, 2, 4).reshape(B, T, H, V)
    return o, S_fin


def token_mixing(hn, w_in_l, attn_g_l, lb_l, hgrn_g_l, w_out_l, k_past, v_past, s0):
    B, T, _ = hn.shape
    f32 = jnp.float32
    z = hn @ w_in_l
    split_at = [int(c) for c in np.cumsum(IN_SPLIT_SIZES)[:-1]]
    aq, ak, av, bq, bf, bi, bg = jnp.split(z, split_at, axis=-1)
    aq = aq.reshape(B, T, A_HEADS, A_HEAD_DIM)
    ak = ak.reshape(B, T, A_HEADS, A_HEAD_DIM)
    av = av.reshape(B, T, A_HEADS, A_HEAD_DIM)

    if k_past is None:
        outs = [dilated_window_prompt(aq, ak, av, w, d) for (w, d) in DILATED_PATTERNS]
        keep = min(WIN_MAX, T)
        k_new, v_new = ak[:, T - keep:], av[:, T - keep:]
        s_init = jnp.zeros((B, B_HEADS, B_KEY_DIM, B_VAL_DIM), f32)
    else:
        k_all = jnp.concatenate([k_past.astype(ak.dtype), ak], axis=1)
        v_all = jnp.concatenate([v_past.astype(av.dtype), av], axis=1)
        outs = [dilated_window_sample(aq, k_all, v_all, w, d) for (w, d) in DILATED_PATTERNS]
        k_new, v_new = ak, av
        s_init = s0.astype(f32)
    o_a = rmsnorm(dilated_mixture(outs).reshape(B, T, A_WIDTH), attn_g_l)

    qh = jax.nn.silu(bq.astype(f32)).reshape(B, T, B_HEADS, B_KEY_DIM)
    f = lb_l + (1.0 - lb_l) * jax.nn.sigmoid(bf.astype(f32))
    f = f.reshape(B, T, B_HEADS, B_KEY_DIM)
    log_f = jnp.log(f)
    kh = 1.0 - f
    vh = bi.astype(f32).reshape(B, T, B_HEADS, B_VAL_DIM)
    chunk = HGRN_CHUNK if T % HGRN_CHUNK == 0 else T
    o_b, s_fin = hgrn2_scan(qh, kh, vh, log_f, s_init, chunk)
    o_b = rmsnorm(o_b, hgrn_g_l) * jax.nn.silu(bg.astype(f32).reshape(B, T, B_HEADS, B_VAL_DIM))

    o = jnp.concatenate([o_a, o_b.reshape(B, T, B_VAL_WIDTH)], axis=-1).astype(hn.dtype)
    return o @ w_out_l, k_new, v_new, s_fin


def setup_inputs(seed: int = 0) -> dict:
    key = jax.random.key(seed)
    ks = iter(jax.random.split(key, 32))
    f32 = jnp.float32
    win_buf = min(WIN_MAX, PAST_LEN)

    def w(shape, fan_in):
        return jax.random.normal(next(ks), shape, f32) * fan_in ** -0.5

    def gain(shape):
        return 1.0 + 0.02 * jax.random.normal(next(ks), shape, f32)

    return {
        'x_prompt': jax.random.normal(next(ks), (BATCH, SEQ, D_MODEL), f32),
        'x_sample': jax.random.normal(next(ks), (DEC_BATCH, DEC_SEQ, D_MODEL), f32),
        'cache_attn_k': jax.random.normal(next(ks), (DEPTH, DEC_BATCH, win_buf, A_HEADS, A_HEAD_DIM), f32),
        'cache_attn_v': jax.random.normal(next(ks), (DEPTH, DEC_BATCH, win_buf, A_HEADS, A_HEAD_DIM), f32),
        'state_hgrn': 0.3 * jax.random.normal(next(ks), (DEPTH, DEC_BATCH, B_HEADS, B_KEY_DIM, B_VAL_DIM), f32),
        'ff1_pre_g': gain((DEPTH, D_MODEL)),
        'ff1_w_gate': w((DEPTH, D_MODEL, D_FF), D_MODEL),
        'ff1_w_up': w((DEPTH, D_MODEL, D_FF), D_MODEL),
        'ff1_w_down': w((DEPTH, D_FF, D_MODEL), D_FF),
        'ff1_post_g': gain((DEPTH, D_MODEL)),
        'mix_pre_g': gain((DEPTH, D_MODEL)),
        'w_in': w((DEPTH, D_MODEL, IN_WIDTH), D_MODEL),
        'attn_norm_g': gain((DEPTH, A_WIDTH)),
        'hgrn_lb_logits': 0.5 * jax.random.normal(next(ks), (DEPTH, B_KEY_WIDTH), f32),
        'hgrn_norm_g': gain((DEPTH, B_VAL_DIM)),
        'w_out': w((DEPTH, MIX_WIDTH, D_MODEL), MIX_WIDTH),
        'mix_post_g': gain((DEPTH, D_MODEL)),
        'ff2_pre_g': gain((DEPTH, D_MODEL)),
        'ff2_w_gate': w((DEPTH, D_MODEL, D_FF), D_MODEL),
        'ff2_w_up': w((DEPTH, D_MODEL, D_FF), D_MODEL),
        'ff2_w_down': w((DEPTH, D_FF, D_MODEL), D_FF),
        'ff2_post_g': gain((DEPTH, D_MODEL)),
    }


def reference(x_prompt, x_sample, cache_attn_k, cache_attn_v, state_hgrn,
              ff1_pre_g, ff1_w_gate, ff1_w_up, ff1_w_down, ff1_post_g,
              mix_pre_g, w_in, attn_norm_g, hgrn_lb_logits, hgrn_norm_g, w_out, mix_post_g,
              ff2_pre_g, ff2_w_gate, ff2_w_up, ff2_w_down, ff2_post_g):
    lb_soft = jax.nn.softmax(hgrn_lb_logits.astype(jnp.float32), axis=0)
    lower_bounds = jnp.cumsum(lb_soft, axis=0) - lb_soft[0]

    def layer(x, l, k_past, v_past, s0):
        h = x + 0.5 * rmsnorm(swiglu(rmsnorm(x, ff1_pre_g[l]), ff1_w_gate[l], ff1_w_up[l], ff1_w_down[l]),
                              ff1_post_g[l])
        m, k_new, v_new, s_new = token_mixing(rmsnorm(h, mix_pre_g[l]), w_in[l], attn_norm_g[l],
                                              lower_bounds[l], hgrn_norm_g[l], w_out[l], k_past, v_past, s0)
        h = h + rmsnorm(m, mix_post_g[l])
        h = h + 0.5 * rmsnorm(swiglu(rmsnorm(h, ff2_pre_g[l]), ff2_w_gate[l], ff2_w_up[l], ff2_w_down[l]),
                              ff2_post_g[l])
        return h, k_new, v_new, s_new

    hp, hs = x_prompt, x_sample
    kp_l, vp_l, sp_l, ks_l, vs_l, ss_l = [], [], [], [], [], []
    for l in range(DEPTH):
        hp, kp, vp, sp = layer(hp, l, None, None, None)
        hs, kd, vd, sd = layer(hs, l, cache_attn_k[l], cache_attn_v[l], state_hgrn[l])
        kp_l.append(kp); vp_l.append(vp); sp_l.append(sp)
        ks_l.append(kd); vs_l.append(vd); ss_l.append(sd)
    new_k_prompt = jnp.stack(kp_l)
    new_v_prompt = jnp.stack(vp_l)
    new_state_prompt = jnp.stack(sp_l)
    new_k_sample = jnp.stack(ks_l)
    new_v_sample = jnp.stack(vs_l)
    new_state_sample = jnp.stack(ss_l)
    return (hp, hs, new_k_prompt, new_v_prompt, new_state_prompt, new_k_sample, new_v_sample, new_state_sample)
```

```python
import numpy as np
import ml_dtypes
from contextlib import ExitStack
import concourse.bass as bass
import concourse.mybir as mybir
from concourse.bass_utils import run_bass_kernel_spmd

F32 = mybir.dt.float32
BF16 = mybir.dt.bfloat16
AF = mybir.ActivationFunctionType
ALU = mybir.AluOpType

D = 1024
DFF = 2816
NFC = 22
INW = 3584
NT_P = 32
NT = 34
NLOC = 18
EPS = 1e-6
EPOCH = 12000
DBG = {}
PATTERNS = ((128, 1), (512, 4), (2048, 16))


class Buf:
    __slots__ = ("lw", "rd", "name", "excl")

    def __init__(self, name="", excl=False):
        self.lw = []
        self.rd = []
        self.name = name
        self.excl = excl


class Op:
    __slots__ = ("fn", "deps", "tok", "dma", "ms", "msnum", "red")

    def __init__(self, fn, deps, tok, dma):
        self.fn = fn
        self.deps = deps
        self.tok = tok
        self.dma = dma
        self.ms = False
        self.msnum = 0
        self.red = None


class Eng:
    def __init__(self, name, is_pe=False):
        self.name = name
        self.ops = []
        self.pending = []
        self.is_pe = is_pe
        self.sems = []
        self.ring = []
        self.ndma = 0


class Tracker:
    def __init__(self):
        self.engs = {}
        self.dma_toks = []
        self.ccq = Eng("ccq")

    def eng(self, name, is_pe=False):
        e = Eng(name, is_pe)
        self.engs[name] = e
        return e

    def record(self, eng, fn, r=(), w=(), dma=False, cc=False):
        deps = set(eng.pending)
        eng.pending = []
        if any(b.excl for b in r):
            w = list(w) + [b for b in r if b.excl and b not in w]
            r = [b for b in r if not b.excl]
        is_dma = dma or cc
        for b in r:
            deps.update(b.lw)
        for b in w:
            for lw_ in b.lw:
                if not (is_dma and lw_[0] == "d" and not b.rd):
                    deps.add(lw_)
            deps.update(b.rd)
        idx = len(eng.ops)
        if cc:
            q = self.ccq
            n = q.ndma
            q.ndma += 1
            tok = ("d", n, 1, q)
            self.dma_toks.append(tok)
            dma = True
        elif dma:
            n = eng.ndma
            eng.ndma += 1
            K = 12
            slot = n % K
            val = 16 * (n // K + 1)
            if n >= K:
                deps.add(("d", slot, val - 16, eng))
            tok = ("d", slot, val, eng)
            self.dma_toks.append(tok)
        else:
            tok = ("c", eng, idx)
        eng.ops.append(Op(fn, deps, tok, dma))
        ws = set(id(b) for b in w)
        for b in w:
            if dma and not b.rd and b.lw and all(x[0] == "d" for x in b.lw):
                b.lw = b.lw + [tok]
            else:
                b.lw = [tok]
            b.rd = []
        for b in r:
            if id(b) not in ws:
                b.rd.append(tok)
        return tok

    def barrier(self):
        toks = []
        for e in self.engs.values():
            if e.ops:
                last = None
                for i in range(len(e.ops) - 1, -1, -1):
                    if not e.ops[i].dma:
                        last = i
                        break
                if last is not None:
                    toks.append(("c", e, last))
        toks.extend(self.dma_toks)
        self.dma_toks = []
        for e in self.engs.values():
            e.pending = list(set(e.pending) | set(toks))

    def finalize(self):
        for e in self.engs.values():
            known_c = {}
            known_d = {}
            for op in e.ops:
                cmax = {}
                dmax = {}
                for dep in op.deps:
                    if dep[0] == "c":
                        if dep[1] is e and e.is_pe:
                            continue
                        k = dep[1].name
                        if dep[2] > cmax.get(k, (-1, None))[0]:
                            cmax[k] = (dep[2], dep[1])
                    else:
                        k = (dep[3].name, dep[1])
                        if dep[2] > dmax.get(k, (-1, None))[0]:
                            dmax[k] = (dep[2], dep)
                red = []
                for k, (i, de) in cmax.items():
                    if known_c.get(k, -1) >= i:
                        continue
                    known_c[k] = i
                    red.append(("c", de, i))
                    de.ops[i].ms = True
                for k, (v, dep) in dmax.items():
                    if known_d.get(k, -1) >= v:
                        continue
                    known_d[k] = v
                    red.append(dep)
                op.red = red
            e.final_red = []
            cmax = {}
            dmax = {}
            for dep in e.pending:
                if dep[0] == "c":
                    if dep[1] is e:
                        continue
                    k = dep[1].name
                    if dep[2] > cmax.get(k, (-1, None))[0]:
                        cmax[k] = (dep[2], dep[1])
                else:
                    k = (dep[3].name, dep[1])
                    if dep[2] > dmax.get(k, (-1, None))[0]:
                        dmax[k] = (dep[2], dep)
            for k, (i, de) in cmax.items():
                if known_c.get(k, -1) >= i:
                    continue
                e.final_red.append(("c", de, i))
                de.ops[i].ms = True
            for k, (v, dep) in dmax.items():
                if known_d.get(k, -1) >= v:
                    continue
                e.final_red.append(dep)
        for e in self.engs.values():
            m = 0
            for op in e.ops:
                if op.ms and not op.dma:
                    m += 1
                    op.msnum = m
            e.n_ms = m

    def n_epochs(self, e):
        return max(1, (e.n_ms + EPOCH - 1) // EPOCH)

    def emit_wait(self, h, dep):
        if dep[0] == "c":
            tgt = dep[1].ops[dep[2]]
            m = tgt.msnum
            h.wait_ge(dep[1].sems[(m - 1) // EPOCH], (m - 1) % EPOCH + 1)
        else:
            h.wait_ge(dep[3].ring[dep[1]], dep[2])

    def replay(self, e, h):
        for op in e.ops:
            for dep in op.red:
                self.emit_wait(h, dep)
            ins = op.fn(h)
            if op.dma:
                if op.tok[3] is self.ccq:
                    ins.then_inc(self.ccq.ring[op.tok[1]])
                else:
                    ins.then_inc(e.ring[op.tok[1]], 16)
            elif op.ms:
                m = op.msnum
                ins.then_inc(e.sems[(m - 1) // EPOCH], 1)
        for dep in e.final_red:
            self.emit_wait(h, dep)


def _mult(diff):
    m = np.zeros(diff.shape, np.float32)
    for (w, d) in PATTERNS:
        m += ((diff >= 0) & (diff % d == 0) & (diff // d <= w // d)).astype(np.float32)
    return m


def make_consts():
    i = np.arange(128)[:, None, None]
    rel = np.arange(18)[None, :, None]
    j = np.arange(128)[None, None, :]
    amask = _mult((16 - rel) * 128 + j - i)
    ii = np.arange(128)[:, None, None, None]
    kt = np.arange(18)[None, :, None, None]
    t = np.arange(8)[None, None, None, :]
    diff = 2048 + t - (kt * 128 + ii) + np.zeros((1, 1, 8, 1), np.int64)
    sm = _mult(diff)
    s = (np.arange(128) % 64)[:, None, None]
    tn = np.arange(8)[None, None, :]
    dn = tn - s + np.zeros((1, 8, 1), np.int64)
    newm = _mult(dn) * (s < 8)
    lo = (np.arange(128) < 64)[:, None, None]
    sm[:, 16] = newm * lo
    sm[:, 17] = newm * (1 - lo)
    hm = ((np.arange(128) % 64)[:, None] <= np.arange(64)[None, :]).astype(np.float32)
    rmask = np.ones((128, 512), np.float32)
    rmask[:, ::64] = 0.0
    bf = ml_dtypes.bfloat16
    return {
        "c_amask": amask.astype(bf),
        "c_smask": sm.astype(bf),
        "c_hmask": hm.astype(bf),
        "c_rmask": rmask,
        "c_ident": np.eye(128, dtype=np.float32).astype(bf),
        "c_pmask": np.stack([(np.arange(128) < 64), (np.arange(128) >= 64)], 1).astype(np.float32),
    }


def build_nc(stages=None):
    nc = bass.Bass("TRN2", target_bir_lowering=False)

    def din(name, shape, dt=F32):
        return nc.dram_tensor(name, list(shape), dt, kind="ExternalInput").ap()

    def dout(name, shape, dt=F32):
        return nc.dram_tensor(name, list(shape), dt, kind="ExternalOutput").ap()

    xin = din("xin", [NLOC * 128, D])
    ck = din("ck", [2, 4, 2048, 512])
    cv = din("cv", [2, 4, 2048, 512])
    sh = din("sh", [2, 4, 4, 128, 128])
    Wd_ = {}
    for nm, shp in [("ff1_pre_g", [2, D]), ("ff1_w_gate", [2, D, DFF]), ("ff1_w_up", [2, D, DFF]),
                    ("ff1_w_down", [2, DFF, D]), ("ff1_post_g", [2, D]), ("mix_pre_g", [2, D]),
                    ("w_in", [2, D, INW]), ("attn_norm_g", [2, 512]), ("hgrn_lb_logits", [2, 512]),
                    ("hgrn_norm_g", [2, 128]), ("w_out", [2, D, D]), ("mix_post_g", [2, D]),
                    ("ff2_pre_g", [2, D]), ("ff2_w_gate", [2, D, DFF]), ("ff2_w_up", [2, D, DFF]),
                    ("ff2_w_down", [2, DFF, D]), ("ff2_post_g", [2, D])]:
        Wd_[nm] = din(nm, shp)
    c_amask = din("c_amask", [128, 18, 128], BF16)
    c_smask = din("c_smask", [128, 18, 8, 8], BF16)
    c_hmask = din("c_hmask", [128, 64], BF16)
    c_rmask = din("c_rmask", [128, 512], F32)
    c_ident = din("c_ident", [128, 128], BF16)
    c_pmask = din("c_pmask", [128, 2], F32)

    yout = dout("yout", [NLOC * 128, D])
    kout = dout("kout", [2, 18 * 128, 512])
    vout = dout("vout", [2, 18 * 128, 512])
    spo = dout("spo", [2, 4, 128, 128])
    sso = dout("sso", [2, 4, 4, 128, 128])
    R = nc.dram_tensor("Rscratch", [NLOC * 128, D], F32).ap()
    Ssend = [nc.dram_tensor(f"Ssend{i}", [256, D], F32).ap() for i in range(8)]
    Srecv = [nc.dram_tensor(f"Srecv{i}", [512, D], F32).ap() for i in range(8)]

    scr = {}
    conv_pending = []

    def plan_conv(key, wdr, rows, cols, colblk):
        t_ = nc.dram_tensor("scr_" + key, [rows, cols], BF16).ap()
        scr[key] = t_
        for r0 in range(0, rows, 128):
            for c0 in range(0, cols, colblk):
                conv_pending.append((key, t_[r0:r0 + 128, c0:c0 + colblk], wdr[r0:r0 + 128, c0:c0 + colblk]))

    def plan_ffn(pre, l):
        plan_conv(f"{pre}g{l}", Wd_[pre + "_w_gate"][l], D, DFF, 1408)
        plan_conv(f"{pre}u{l}", Wd_[pre + "_w_up"][l], D, DFF, 1408)
        plan_conv(f"{pre}d{l}", Wd_[pre + "_w_down"][l], DFF, D, 1024)

    def plan_mix(l):
        plan_conv(f"win{l}", Wd_["w_in"][l], D, INW, 1792)
        plan_conv(f"wout{l}", Wd_["w_out"][l], D, D, 1024)

    def plan_cache(l):
        for nm, cdr in (("cv", cv), ("ck", ck)):
            for j in range(4):
                key = f"{nm}{l}_{j}"
                t_ = nc.dram_tensor("scr_" + key, [2048, 512], BF16).ap()
                scr[key] = t_
                for q4 in range(4):
                    conv_pending.append((key, t_[q4 * 512:(q4 + 1) * 512, :], cdr[l, j, q4 * 512:(q4 + 1) * 512, :]))

    def pump(n):
        for _ in range(n):
            if not conv_pending:
                return
            key, o_, i_ = conv_pending.pop(0)
            T.record(POOL, mk("dma_start", out=o_, in_=i_), r=[], w=[bf("scr_" + key)], dma=True)

    def pump_until(keys):
        while any(k_[0] in keys for k_ in conv_pending):
            pump(1)

    def loc_R(Rt):
        return lambda i: Rt[i * 128:(i + 1) * 128, :]

    def loc_S(i):
        return Ssend[i // 2][(i % 2) * 128:(i % 2) * 128 + 128, :] if i < 16 else R[i * 128:(i + 1) * 128, :]
    c_flag = din("c_flag", [128, 1], F32)
    dbg = dout("dbg", [128, 8192]) if stages is not None else None

    off = [16512]

    def sb(name, shape, dt, at=None):
        nbytes = int(np.prod(shape[1:])) * (4 if dt == F32 else 2)
        if at is None:
            at = off[0]
            off[0] = (at + nbytes + 31) // 32 * 32
        return nc.alloc_sbuf_tensor_at(name, list(shape), dt, offset=at), at, nbytes

    ident, _, _ = sb("ident", [128, 128], BF16)
    onesc, _, _ = sb("onesc", [128, 2], BF16)
    stat, _, _ = sb("stat", [128, 64], F32)
    lbc, _, _ = sb("lbc", [128, 2, 4], F32)
    omlb, _, _ = sb("omlb", [128, 2, 4], F32)
    lgt, _, _ = sb("lgt", [128, 2, 4], F32)
    pmask, _, _ = sb("pmask", [128, 2], F32)
    zer, _, _ = sb("zer", [128, 512], BF16)
    flagc, _, _ = sb("flagc", [128, 1], F32)
    ones8, _, _ = sb("ones8", [128, 8], BF16)
    junk, _, _ = sb("junk", [128, 1024], BF16)
    gA, _, _ = sb("gA", [128, 1024], F32)
    gB, _, _ = sb("gB", [128, 1024], F32)
    xbuf, xbuf_at, _ = sb("xbuf", [128, 2, 2, 1024], F32)
    xn, _, _ = sb("xn", [128, 2, 1024], BF16)
    xnT, xnT_at, _ = sb("xnT", [128, 2, 8, 256], BF16)
    t1, t1_at, _ = sb("t1", [128, 2, 1024], F32)
    base = off[0]
    off[0] = base
    Wg, _, _ = sb("Wg", [128, 8, DFF], BF16)
    Wu, _, _ = sb("Wu", [128, 8, DFF], BF16)
    Wdn, _, _ = sb("Wdn", [128, NFC, D], BF16)
    hT, _, _ = sb("hT", [128, 2, NFC, 256], BF16)
    sg, _, _ = sb("sg", [128, 3, 256], F32)
    ffn_end = off[0]
    off[0] = base
    Win, _, _ = sb("Win", [128, 8, INW], BF16)
    Wout, _, _ = sb("Wout", [128, 8, D], BF16)
    gag, _, _ = sb("gag", [128, 512], F32)
    ghg, _, _ = sb("ghg", [128, 512], F32)
    amask, _, _ = sb("amask", [128, 18, 128], BF16)
    smask, _, _ = sb("smask", [128, 18, 8, 8], BF16)
    hmask, _, _ = sb("hmask", [128, 64], BF16)
    rmask, _, _ = sb("rmask", [128, 512], F32)
    qT, _, _ = sb("qT", [128, 2, 2, 4, 128], BF16)
    ksb, _, _ = sb("ksb", [128, 2, 512], F32, at=xbuf_at + 8192)
    vsb, _, _ = sb("vsb", [128, 2, 512], F32, at=xbuf_at + 8192 + 4096)
    qs, _, _ = sb("qs", [128, 512], F32)
    fT, _, _ = sb("fT", [128, 512], F32)
    GT, _, _ = sb("GT", [128, 512], F32)
    eG, _, _ = sb("eG", [128, 512], F32)
    eGn, _, _ = sb("eGn", [128, 512], F32)
    eGl, _, _ = sb("eGl", [128, 8], F32)
    QgT, _, _ = sb("QgT", [128, 512], BF16)
    KdT, _, _ = sb("KdT", [128, 512], BF16)
    KeT, _, _ = sb("KeT", [128, 512], BF16)
    Kesb, _, _ = sb("Kesb", [128, 512], BF16)
    vh, _, _ = sb("vh", [128, 512], BF16)
    sgate, _, _ = sb("sgate", [128, 512], F32, at=t1_at + 4096 + 2048)
    AT, _, _ = sb("AT", [128, 4, 64], BF16)
    Sst, _, _ = sb("Sst", [128, 4, 128], F32)
    Sbf, _, _ = sb("Sbf", [128, 4, 128], BF16)
    osb, _, _ = sb("osb", [128, 8, 65], F32)
    rz, _, _ = sb("rz", [128, 8], F32)
    omix, _, _ = sb("omix", [128, 1024], BF16, at=xnT_at + 4096)
    omT, _, _ = sb("omT", [128, 8, 128], BF16, at=xnT_at + 4096 + 2048)
    Pt, _, _ = sb("Pt", [128, 4, 4, 128], BF16)
    kvbase = off[0]
    kTr, _, _ = sb("kTr", [128, 18, 4, 128], BF16)
    Vp, _, _ = sb("Vp", [128, 18, 8, 66], BF16)
    kv_end = off[0]
    off[0] = kvbase
    kc, _, _ = sb("kc", [128, 4, 512], BF16)
    kcT, _, _ = sb("kcT", [128, 4, 2048], BF16)
    vc, _, _ = sb("vc", [128, 16, 512], BF16)
    Ps, _, _ = sb("Ps", [128, 17, 8, 8], BF16)
    ksT, _, _ = sb("ksT", [128, 4, 128], BF16)
    vsbf, _, _ = sb("vsbf", [128, 512], BF16)
    mix_end = max(off[0], kv_end)
    assert ffn_end <= 229376 and mix_end <= 229376, (ffn_end, mix_end)

    psA = [nc.alloc_psum_tensor(f"psA{i}", [128, 512], F32) for i in range(6)]
    psT = [nc.alloc_psum_tensor(f"psT{i}", [128, 1024], BF16) for i in range(2)]
    psAB = [Buf(f"psA{i}", excl=True) for i in range(6)]
    psTB = [Buf(f"psT{i}", excl=True) for i in range(2)]
    pa_ctr = [0]
    pt_ctr = [0]

    def nextA(exclude=()):
        while True:
            i = pa_ctr[0] % 6
            pa_ctr[0] += 1
            if i not in exclude:
                return psA[i], psAB[i]

    def nextT():
        i = pt_ctr[0] % 2
        pt_ctr[0] += 1
        return psT[i], psTB[i]

    T = Tracker()
    PE = T.eng("pe", is_pe=True)
    ACT = T.eng("act")
    DVE = T.eng("dve")
    POOL = T.eng("pool")
    SP = T.eng("sp")

    def mk(name, *a, **kw):
        return lambda h: getattr(h, name)(*a, **kw)

    def op(e, fn, r=(), w=()):
        return T.record(e, fn, r, w)

    def dma(e, out, in_, r=(), w=(), **kw):
        return T.record(e, mk("dma_start", out=out, in_=in_, **kw), r, w, dma=True)

    stat_ctr = [0]
    statB = [Buf(f"stat{i}") for i in range(16)]

    def next_stat():
        i = stat_ctr[0] % 16
        stat_ctr[0] += 1
        return stat[:, i * 4:(i + 1) * 4], statB[i]

    junkB = Buf("junk")

    def bcast_row(ap_row, n):
        return bass.AP(ap_row.tensor, ap_row.offset, [[0, 128], [1, n]])

    B = {}

    def bf(name):
        if name not in B:
            B[name] = Buf(name)
        return B[name]

    dma(SP, ident[:, :], c_ident[:, :], w=[bf("ident")])
    op(POOL, mk("memset", onesc[:, :], 1.0), w=[bf("onesc")])
    dma(SP, pmask[:, :], c_pmask[:, :], w=[bf("pmask")])
    op(POOL, mk("memset", zer[:, :], 0.0), w=[bf("zer")])
    op(POOL, mk("memset", ones8[:, :], 1.0), w=[bf("ones8")])
    dma(SP, flagc[:, :], c_flag[:, :], w=[bf("flagc")])
    op(POOL, mk("memset", lbc[:, :, :], 0.0), w=[bf("lbc")])
    for l in range(2):
        for hh in range(4):
            src = Wd_["hgrn_lb_logits"][l, hh * 128:(hh + 1) * 128].rearrange("(p o) -> p o", o=1)
            dma(SP, lgt[:, l, hh:hh + 1], src, w=[bf("lgt")])
    op(DVE, mk("tensor_tensor", out=lgt[:, 0, :], in0=lgt[:, 1, :], in1=lgt[:, 0, :], op=ALU.subtract),
       r=[bf("lgt")], w=[bf("lgt")])
    op(ACT, mk("activation", out=lbc[:, 1, :], in_=lgt[:, 0, :], func=AF.Sigmoid),
       r=[bf("lgt")], w=[bf("lbc")])
    op(DVE, mk("tensor_scalar", out=omlb[:, :, :], in0=lbc[:, :, :], scalar1=-1.0, scalar2=1.0,
                                      op0=ALU.mult, op1=ALU.add), r=[bf("lbc")], w=[bf("omlb")])

    def rstd_chain(src_ap, tmp_ap, dst_ap, sB, a, b):
        op(DVE, mk("tensor_scalar", out=tmp_ap, in0=src_ap, scalar1=a, scalar2=b,
                                          op0=ALU.mult, op1=ALU.add), r=[sB], w=[sB])
        op(ACT, mk("activation", out=tmp_ap, in_=tmp_ap, func=AF.Sqrt), r=[sB], w=[sB])
        op(DVE, mk("reciprocal", out=dst_ap, in_=tmp_ap), r=[sB], w=[sB])

    def rmsnorm_to_bf16(x_ap, xB, g_ap, gBuf, out_ap, outB, dim, scale_mul=1.0):
        st, sB = next_stat()
        op(ACT, mk("activation", out=junk[:, 0:dim], in_=x_ap, func=AF.Square, accum_out=st[:, 0:1]),
           r=[xB], w=[junkB, sB])
        rstd_chain(st[:, 0:1], st[:, 1:2], st[:, 2:3], sB, 1.0 / dim, EPS)
        op(DVE, mk("scalar_tensor_tensor", out=out_ap, in0=x_ap, scalar=st[:, 2:3], in1=g_ap,
                                                 op0=ALU.mult, op1=ALU.mult), r=[xB, sB, gBuf], w=[outB])

    def transpose_to(src_ap_fn, srcB, nchunks, dst_ap, dstB):
        pt, ptB = nextT()
        for c in range(nchunks):
            op(PE, (lambda c: mk("transpose", out=pt[:, c * 128:(c + 1) * 128], in_=src_ap_fn(c),
                                                    identity=ident[:, :]))(c),
               r=[srcB, bf("ident")], w=[ptB])
        op(ACT, mk("activation", out=dst_ap,
                                       in_=pt[:, 0:nchunks * 128].rearrange("p (c n) -> p c n", c=nchunks),
                                       func=AF.Copy), r=[ptB], w=[dstB])

    def tile_row(t):
        return slice(t * 128, (t + 1) * 128)

    def ffn_stage(l, pre, src, dst):
        T.barrier()
        wgd, wud, wdd = Wd_[pre + "_w_gate"][l], Wd_[pre + "_w_up"][l], Wd_[pre + "_w_down"][l]
        dma(SP, gA[:, :], bcast_row(Wd_[pre + "_pre_g"][l], D), w=[bf("gA")])
        dma(SP, gB[:, :], bcast_row(Wd_[pre + "_post_g"][l], D), w=[bf("gB")])
        pre_conv = (f"{pre}g{l}" in scr)
        if pre_conv:
            pump_until((f"{pre}g{l}", f"{pre}u{l}", f"{pre}d{l}"))
            wgd, wud, wdd = scr[f"{pre}g{l}"], scr[f"{pre}u{l}"], scr[f"{pre}d{l}"]
        q_ = SP if pre_conv else POOL
        for cb in range(2):
            for (wsb, wdr, nm, sk) in ((Wg, wgd, "Wg", f"scr_{pre}g{l}"), (Wu, wud, "Wu", f"scr_{pre}u{l}")):
                for kcx in range(8):
                    dma(q_, wsb[:, kcx, cb * 1408:(cb + 1) * 1408],
                        wdr[kcx * 128:(kcx + 1) * 128, cb * 1408:(cb + 1) * 1408],
                        r=[bf(sk)] if pre_conv else [], w=[bf(f"{nm}{cb}")])
        for fc in range(NFC):
            dma(q_, Wdn[:, fc, :], wdd[fc * 128:(fc + 1) * 128, :],
                r=[bf(f"scr_{pre}d{l}")] if pre_conv else [], w=[bf(f"Wd{fc // 11}")])
        ngroups = NLOC // 2

        def load_group(g):
            for ti in range(2):
                dma(SP, xbuf[:, g % 2, ti, :], src(2 * g + ti), w=[bf(f"xbuf{g % 2}")])

        load_group(0)
        for g in range(ngroups):
            s = g % 2
            XB = bf(f"xbuf{s}")
            if g + 1 < ngroups:
                load_group(g + 1)
            xnTB = bf(f"xnT{s}")
            for ti in range(2):
                xnB = bf(f"xn{ti}")
                rmsnorm_to_bf16(xbuf[:, s, ti, :], XB, gA[:, :], bf("gA"), xn[:, ti, :], xnB, D)
                transpose_to((lambda ti: lambda c: xn[:, ti, c * 128:(c + 1) * 128])(ti), xnB, 8,
                             xnT[:, s, :, ti * 128:(ti + 1) * 128], xnTB)
            hB = bf(f"hT{s}")
            for fc in range(NFC):
                wB = [bf(f"Wg{fc // 11}"), bf(f"Wu{fc // 11}")]
                pg, pgB = nextA()
                pu, puB = nextA()
                for kcx in range(8):
                    op(PE, (lambda kcx, fc, pg: mk("matmul",
                        pg[:, 0:256], lhsT=Wg[:, kcx, fc * 128:(fc + 1) * 128], rhs=xnT[:, s, kcx, :],
                        start=(kcx == 0), stop=(kcx == 7)))(kcx, fc, pg), r=[wB[0], xnTB], w=[pgB])
                for kcx in range(8):
                    op(PE, (lambda kcx, fc, pu: mk("matmul",
                        pu[:, 0:256], lhsT=Wu[:, kcx, fc * 128:(fc + 1) * 128], rhs=xnT[:, s, kcx, :],
                        start=(kcx == 0), stop=(kcx == 7)))(kcx, fc, pu), r=[wB[1], xnTB], w=[puB])
                si = fc % 3
                sgB = bf(f"sg{si}")
                op(ACT, (lambda pg, si: mk("activation", out=sg[:, si, :], in_=pg[:, 0:256], func=AF.Silu))(pg, si),
                   r=[pgB], w=[sgB])
                op(DVE, (lambda pu, si, fc: mk("tensor_tensor",
                    out=hT[:, s, fc, :], in0=sg[:, si, :], in1=pu[:, 0:256], op=ALU.mult))(pu, si, fc),
                   r=[sgB, puB], w=[hB])
            for ti in range(2):
                py = [nextA(), nextA()]
                for dh in range(2):
                    for fc in range(NFC):
                        op(PE, (lambda dh, fc, p: mk("matmul",
                            p[:, :], lhsT=hT[:, s, fc, ti * 128:(ti + 1) * 128],
                            rhs=Wdn[:, fc, dh * 512:(dh + 1) * 512],
                            start=(fc == 0), stop=(fc == NFC - 1)))(dh, fc, py[dh][0]),
                           r=[hB, bf(f"Wd{fc // 11}")], w=[py[dh][1]])
                st, sB = next_stat()
                for dh in range(2):
                    op(ACT, (lambda dh, p: mk("activation",
                        out=junk[:, 0:512], in_=p[:, :], func=AF.Square, accum_out=st[:, dh:dh + 1]))(dh, py[dh][0]),
                       r=[py[dh][1]], w=[junkB, sB])
                op(DVE, mk("tensor_tensor", out=st[:, 2:3], in0=st[:, 0:1], in1=st[:, 1:2], op=ALU.add),
                   r=[sB], w=[sB])
                rstd_chain(st[:, 2:3], st[:, 3:4], st[:, 2:3], sB, 4.0 / D, 4.0 * EPS)
                tB = bf(f"t1{ti}")
                for dh in range(2):
                    op(DVE, (lambda dh, p: mk("scalar_tensor_tensor",
                        out=t1[:, ti, dh * 512:(dh + 1) * 512], in0=p[:, :], scalar=st[:, 2:3],
                        in1=gB[:, dh * 512:(dh + 1) * 512], op0=ALU.mult, op1=ALU.mult))(dh, py[dh][0]),
                       r=[py[dh][1], sB, bf("gB")], w=[tB])
                op(DVE, mk("tensor_tensor", out=xbuf[:, s, ti, :], in0=xbuf[:, s, ti, :], in1=t1[:, ti, :],
                                                   op=ALU.add), r=[tB, XB], w=[XB])
                dma(SP, dst(2 * g + ti), xbuf[:, s, ti, :], r=[XB])
            pump(6)

    CQ, CK, CV, CBQ, CBF, CBI, CBG = 0, 512, 1024, 1536, 2048, 2560, 3072

    def mix_stage(l, src, dst):
        T.barrier()
        if not DBG.get("no_cc"):
            for i in range(8):
                T.record(POOL, mk("collective_compute", "AllGather", ALU.bypass,
                                  replica_groups=[[0, 1], [2, 3], [4, 5], [6, 7]],
                                  ins=[Ssend[i][:, :]], outs=[Srecv[i][:, :]]), r=[], w=[bf(f"Srecv{i}")], cc=True)

        def src_rows(t):
            if t < 16:
                return Srecv[t // 2][(t % 2) * 128:(t % 2) * 128 + 128, :]
            return src(t - 16)

        def src_bufs(t):
            return [bf(f"Srecv{t // 2}")] if t < 16 else []

        def dst_rows(t):
            return dst(t - 16)

        wind, woutd = Wd_["w_in"][l], Wd_["w_out"][l]
        dma(SP, gA[:, :], bcast_row(Wd_["mix_pre_g"][l], D), w=[bf("gA")])
        dma(SP, gB[:, :], bcast_row(Wd_["mix_post_g"][l], D), w=[bf("gB")])
        dma(SP, gag[:, :], bcast_row(Wd_["attn_norm_g"][l], 512), w=[bf("gag")])
        for hh in range(4):
            dma(SP, ghg[:, hh * 128:(hh + 1) * 128], bcast_row(Wd_["hgrn_norm_g"][l], 128), w=[bf("ghg")])
        dma(SP, amask[:, :, :], c_amask[:, :, :], w=[bf("amask")])
        dma(SP, smask[:, :, :, :], c_smask[:, :, :, :], w=[bf("smask")])
        dma(SP, hmask[:, :], c_hmask[:, :], w=[bf("hmask")])
        dma(SP, rmask[:, :], c_rmask[:, :], w=[bf("rmask")])
        pump_until((f"win{l}", f"wout{l}"))
        wind, woutd = scr[f"win{l}"], scr[f"wout{l}"]
        for cb in range(2):
            for kcx in range(8):
                dma(SP, Win[:, kcx, cb * 1792:(cb + 1) * 1792],
                    wind[kcx * 128:(kcx + 1) * 128, cb * 1792:(cb + 1) * 1792], r=[bf(f"scr_win{l}")], w=[bf("Win")])
        for kcx in range(8):
            dma(SP, Wout[:, kcx, :], woutd[kcx * 128:(kcx + 1) * 128, :], r=[bf(f"scr_wout{l}")], w=[bf("Wout")])
        op(POOL, mk("memset", osb[:, :, :], 1.0), w=[bf("osb")])
        op(POOL, mk("memset", Sst[:, :, :], 0.0), w=[bf("Sst")])
        op(POOL, mk("memset", Sbf[:, :, :], 0.0), w=[bf("Sbf")])

        tiles_ = [NT_P, NT_P + 1] + list(range(NT_P))
        if DBG.get("ntiles"):
            tiles_ = list(range(DBG["ntiles"]))
        if DBG.get("sample_only"):
            tiles_ = [NT_P, NT_P + 1]
        do_attn = not DBG.get("no_attn")
        do_hgrn = not DBG.get("no_hgrn")
        if not (do_attn and do_hgrn):
            op(POOL, mk("memset", omix[:, :], 0.25), w=[bf("omix")])
        if DBG.get("no_tiles"):
            tiles_ = []
        for idx_, t in enumerate(tiles_):
            is_s = t >= NT_P
            if t == 0 and idx_ > 0:
                T.barrier()
                op(POOL, mk("memset", Sst[:, :, :], 0.0), w=[bf("Sst")])
                op(POOL, mk("memset", Sbf[:, :, :], 0.0), w=[bf("Sbf")])
            s = t % 2
            XB = bf(f"xbuf{s}")
            warm = t < 16
            if idx_ == 0:
                dma(SP, xbuf[:, 0, s, :], src_rows(t), r=src_bufs(t), w=[XB])
            if idx_ + 1 < len(tiles_):
                tn_ = tiles_[idx_ + 1]
                dma(SP, xbuf[:, 0, tn_ % 2, :], src_rows(tn_), r=src_bufs(tn_), w=[bf(f"xbuf{tn_ % 2}")])
            xnB = bf("xn0")
            rmsnorm_to_bf16(xbuf[:, 0, s, :], XB, gA[:, :], bf("gA"), xn[:, 0, :], xnB, D)
            xnTB = bf("xnT0")
            transpose_to(lambda c: xn[:, 0, c * 128:(c + 1) * 128], xnB, 8, xnT[:, 0, :, 0:128], xnTB)
            xr = lambda kcx: xnT[:, 0, kcx, 0:128]
            slot = t % 18

            def aform(col0, nch):
                p, pB = nextA()
                for c in range(nch):
                    for kcx in range(8):
                        op(PE, (lambda c, kcx: mk("matmul",
                            p[:, c * 128:(c + 1) * 128], lhsT=Win[:, kcx, col0 + c * 128: col0 + (c + 1) * 128],
                            rhs=xr(kcx), start=(kcx == 0), stop=(kcx == 7)))(c, kcx),
                           r=[bf("Win"), xnTB], w=[pB])
                return p, pB

            def bform(col0):
                p, pB = nextA()
                for kcx in range(8):
                    op(PE, (lambda kcx: mk("matmul",
                        p[:, :], lhsT=xr(kcx), rhs=Win[:, kcx, col0:col0 + 512],
                        start=(kcx == 0), stop=(kcx == 7)))(kcx), r=[bf("Win"), xnTB], w=[pB])
                return p, pB

            qs_ = t % 2
            qB = bf(f"qT{qs_}")
            parts = DBG.get("parts", "qkKv")
            if not warm:
                p, pB = aform(CQ, 4)
            for par in (DBG.get("qpar", [0, 1]) if ("q" in parts and not warm) else []):
                op(DVE, (lambda p, par: mk("tensor_scalar",
                    out=qT[:, qs_, par, :, :], in0=p[:, :].rearrange("p (c n) -> p c n", c=4),
                    scalar1=pmask[:, par:par + 1], scalar2=None, op0=ALU.mult))(p, par),
                   r=[pB, bf("pmask")], w=[qB])
            p, pB = aform(CK, 4)
            if "k" not in parts:
                pass
            elif not is_s:
                kB = bf(f"kT{slot}")
                op(DVE, (lambda p: mk("tensor_copy", out=kTr[:, slot, :, :],
                                                           in_=p[:, :].rearrange("p (c n) -> p c n", c=4)))(p),
                   r=[pB], w=[kB])
            else:
                kB = bf("ksT")
                op(DVE, (lambda p: mk("tensor_copy", out=ksT[:, :, :],
                                                           in_=p[:, :].rearrange("p (c n) -> p c n", c=4)))(p),
                   r=[pB], w=[kB])
            ks_ = t % 2
            ksB = bf(f"ksb{ks_}")
            if not warm:
                p, pB = bform(CK)
            if "K" in parts and not warm:
                op(ACT, (lambda p: mk("activation", out=ksb[:, ks_, :], in_=p[:, :], func=AF.Copy))(p),
                   r=[pB], w=[ksB])
            p, pB = bform(CV)
            vsB = bf(f"vsb{ks_}")
            if "v" in parts and not warm:
                op(ACT, (lambda p: mk("activation", out=vsb[:, ks_, :], in_=p[:, :], func=AF.Copy))(p),
                   r=[pB], w=[vsB])
            if DBG.get("skip_vcopy"):
                pass
            elif warm:
                vB = bf(f"Vp{slot}")
                op(DVE, mk("tensor_scalar", out=Vp[:, slot, :, 0:64], in0=p[:, :].rearrange("p (a d) -> p a d", a=8),
                           scalar1=flagc[:, 0:1], scalar2=None, op0=ALU.mult), r=[pB, bf("flagc")], w=[vB])
                op(DVE, mk("tensor_scalar", out=Vp[:, slot, :, 64:65],
                           in0=ones8[:, :].rearrange("p (a o) -> p a o", o=1),
                           scalar1=flagc[:, 0:1], scalar2=None, op0=ALU.mult), r=[bf("ones8"), bf("flagc")], w=[vB])
            elif not is_s:
                vB = bf(f"Vp{slot}")
                op(ACT, (lambda p: mk("activation",
                    out=Vp[:, slot, :, 0:64], in_=p[:, :].rearrange("p (a d) -> p a d", a=8), func=AF.Copy))(p),
                   r=[pB], w=[vB])
                op(POOL, mk("tensor_copy", out=Vp[:, slot, :, 64:65], in_=ones8[:, :].rearrange("p (a o) -> p a o", o=1)),
                   r=[bf("ones8")], w=[vB])
            else:
                vB = bf("vsbf")
                op(DVE, (lambda p: mk("tensor_copy", out=vsbf[:, :], in_=p[:, :]))(p), r=[pB], w=[vB])
            if t >= 16:
                orow = (t - 16) * 128
                dma(SP, kout[l, orow:orow + 128, :], ksb[:, ks_, :], r=[ksB])
                dma(SP, vout[l, orow:orow + 128, :], vsb[:, ks_, :], r=[vsB])

            if DBG.get("level", 9) < 1:
                continue
            if not warm:
                p, pB = aform(CBQ, 4)
                op(ACT, (lambda p: mk("activation", out=qs[:, :], in_=p[:, :], func=AF.Silu))(p),
                   r=[pB], w=[bf("qs")])
            p, pB = aform(CBF, 4)
            op(ACT, (lambda p: mk("activation", out=fT[:, :], in_=p[:, :], func=AF.Sigmoid))(p),
               r=[pB], w=[bf("fT")])
            for hh in range(4):
                op(DVE, (lambda hh: mk("tensor_scalar",
                    out=fT[:, hh * 128:(hh + 1) * 128], in0=fT[:, hh * 128:(hh + 1) * 128],
                    scalar1=omlb[:, l, hh:hh + 1], scalar2=lbc[:, l, hh:hh + 1],
                    op0=ALU.mult, op1=ALU.add))(hh), r=[bf("fT"), bf("omlb"), bf("lbc")], w=[bf("fT")])
            op(ACT, mk("activation", out=eG[:, :], in_=fT[:, :], func=AF.Ln), r=[bf("fT")], w=[bf("eG")])
            op(DVE, mk("tensor_tensor_scan", out=GT[:, :], data0=rmask[:, :], data1=eG[:, :], initial=0.0,
                                                   op0=ALU.mult, op1=ALU.add),
               r=[bf("eG"), bf("rmask")], w=[bf("GT")])
            if not warm:
                op(ACT, mk("activation", out=eG[:, :], in_=GT[:, :], func=AF.Exp), r=[bf("GT")], w=[bf("eG")])
            op(ACT, mk("activation", out=eGn[:, :], in_=GT[:, :], func=AF.Exp, scale=-1.0),
               r=[bf("GT")], w=[bf("eGn")])
            lastc = 7 if is_s else 63
            gl_view = GT[:, :].rearrange("p (a n) -> p a n", n=64)[:, :, lastc:lastc + 1]
            op(ACT, mk("activation", out=eGl[:, :].rearrange("p (a o) -> p a o", o=1), in_=gl_view, func=AF.Exp),
               r=[bf("GT")], w=[bf("eGl")])
            if not warm:
                op(DVE, mk("tensor_tensor", out=QgT[:, :], in0=qs[:, :], in1=eG[:, :], op=ALU.mult),
                   r=[bf("qs"), bf("eG")], w=[bf("QgT")])
            op(DVE, mk("tensor_scalar", out=fT[:, :], in0=fT[:, :], scalar1=-1.0, scalar2=1.0,
                                              op0=ALU.mult, op1=ALU.add), r=[bf("fT")], w=[bf("fT")])
            op(DVE, mk("tensor_tensor", out=eGn[:, :], in0=eGn[:, :], in1=fT[:, :], op=ALU.mult),
               r=[bf("fT"), bf("eGn")], w=[bf("eGn")])
            if not warm:
                op(POOL, mk("tensor_copy", out=KdT[:, :], in_=eGn[:, :]), r=[bf("eGn")], w=[bf("KdT")])
            op(DVE, mk("tensor_tensor",
                out=KeT[:, :].rearrange("p (a n) -> p a n", n=64),
                in0=eGn[:, :].rearrange("p (a n) -> p a n", n=64),
                in1=eGl[:, :].rearrange("p (a o) -> p a o", o=1).to_broadcast([128, 8, 64]), op=ALU.mult),
               r=[bf("eGn"), bf("eGl")], w=[bf("KeT")])
            transpose_to(lambda c: KeT[:, c * 128:(c + 1) * 128], bf("KeT"), 4,
                         Kesb[:, :].rearrange("p (c n) -> p c n", c=4), bf("Kesb"))
            p, pB = bform(CBI)
            op(ACT, (lambda p: mk("activation", out=vh[:, :], in_=p[:, :], func=AF.Copy))(p),
               r=[pB], w=[bf("vh")])
            if not warm:
                p, pB = bform(CBG)
                op(ACT, (lambda p: mk("activation", out=sgate[:, :], in_=p[:, :], func=AF.Silu))(p),
                   r=[pB], w=[bf("sgate")])

            if DBG.get("level", 9) < 2:
                continue
            if warm:
                for cix in range(2):
                    pS, pSB = nextA()
                    for hh in range(4):
                        op(PE, mk("matmul", pS[:, hh * 128:(hh + 1) * 128],
                                  lhsT=Kesb[cix * 64:(cix + 1) * 64, hh * 128:(hh + 1) * 128],
                                  rhs=vh[cix * 64:(cix + 1) * 64, hh * 128:(hh + 1) * 128], start=True, stop=True),
                           r=[bf("Kesb"), bf("vh")], w=[pSB])
                    egl_bc = eGl[:, :].rearrange("p (a c) -> p a c", c=2)[:, :, cix:cix + 1].to_broadcast([128, 4, 128])
                    op(DVE, mk("tensor_tensor", out=Sst[:, :, :], in0=Sst[:, :, :], in1=egl_bc, op=ALU.mult),
                       r=[bf("Sst"), bf("eGl")], w=[bf("Sst")])
                    op(DVE, mk("tensor_tensor", out=Sst[:, :, :], in0=Sst[:, :, :],
                               in1=pS[:, :].rearrange("p (a n) -> p a n", a=4), op=ALU.add),
                       r=[bf("Sst"), pSB], w=[bf("Sst")])
                if t == 15:
                    op(DVE, mk("tensor_scalar", out=Sst[:, :, :], in0=Sst[:, :, :], scalar1=flagc[:, 0:1],
                               scalar2=None, op0=ALU.mult), r=[bf("Sst"), bf("flagc")], w=[bf("Sst")])
                    op(ACT, mk("activation", out=Sbf[:, :, :], in_=Sst[:, :, :], func=AF.Copy),
                       r=[bf("Sst")], w=[bf("Sbf")])
                pump(2)
                continue
            osB = bf("osb")
            if not do_attn:
                pass
            elif not is_s:
                kts = list(range(max(0, t - 16), t + 1))
                po = [(psA[0], psAB[0]), (psA[1], psAB[1])]
                items = [(hd, g0) for hd in range(8) for g0 in range(0, len(kts), 4)]
                st_ = {}

                def emit_S(ix):
                    hd, g0 = items[ix]
                    c = hd // 2
                    par = hd % 2
                    grp = kts[g0:g0 + 4]
                    ps_, psB_ = nextA(exclude=(0, 1))
                    st_[ix] = (ps_, psB_)
                    for gi, kt in enumerate(grp):
                        op(PE, (lambda gi, kt, ps_, c, par: mk("matmul",
                            ps_[:, gi * 128:(gi + 1) * 128], lhsT=kTr[:, kt % 18, c, :],
                            rhs=qT[:, qs_, par, c, :], start=True, stop=True))(gi, kt, ps_, c, par),
                           r=[bf(f"kT{kt % 18}"), qB], w=[psB_])

                def emit_rest(ix):
                    hd, g0 = items[ix]
                    grp = kts[g0:g0 + 4]
                    ps_, psB_ = st_.pop(ix)
                    pO, pOB = po[hd // 4]
                    pi = ix % 4
                    PB = bf(f"Pt{pi}")
                    n = len(grp)
                    rel0 = grp[0] - t + 16
                    op(ACT, (lambda n, pi, ps_: mk("activation",
                        out=Pt[:, pi, 0:n, :], in_=ps_[:, 0:n * 128].rearrange("p (a n) -> p a n", a=n),
                        func=AF.Exp, scale=0.125))(n, pi, ps_), r=[psB_], w=[PB])
                    meng = POOL if ix % 3 == 0 else DVE
                    op(meng, (lambda n, pi, rel0: mk("tensor_tensor",
                        out=Pt[:, pi, 0:n, :], in0=Pt[:, pi, 0:n, :], in1=amask[:, rel0:rel0 + n, :],
                        op=ALU.mult))(n, pi, rel0), r=[PB, bf("amask")], w=[PB])
                    for gi, kt in enumerate(grp):
                        first = (kt == kts[0])
                        last = (kt == kts[-1])
                        op(PE, (lambda gi, kt, first, last, pi, hd, pO: mk("matmul",
                            pO[:, (hd % 4) * 65:(hd % 4) * 65 + 65], lhsT=Pt[:, pi, gi, :],
                            rhs=Vp[:, kt % 18, hd, 0:65], start=first, stop=last))(gi, kt, first, last, pi, hd, pO),
                           r=[PB, bf(f"Vp{kt % 18}")], w=[pOB])

                LA = 3
                for ix in range(min(LA, len(items))):
                    emit_S(ix)
                for ix in range(len(items)):
                    if ix + LA < len(items):
                        emit_S(ix + LA)
                    emit_rest(ix)
                for hh2 in range(2):
                    op(ACT, (lambda hh2: mk("activation",
                        out=osb[:, hh2 * 4:(hh2 + 1) * 4, :],
                        in_=po[hh2][0][:, 0:260].rearrange("p (a d) -> p a d", a=4), func=AF.Copy))(hh2),
                       r=[po[hh2][1]], w=[osB])
            else:
                pO, pOB = psA[0], psAB[0]
                pZ, pZB = psA[1], psAB[1]
                op(PE, mk("matmul", pO[:, :], lhsT=zer[:, 0:128], rhs=zer[:, :], start=True, stop=True),
                   r=[bf("zer")], w=[pOB])
                op(PE, mk("matmul", pZ[:, 0:16], lhsT=zer[:, 0:128], rhs=zer[:, 0:16], start=True, stop=True),
                   r=[bf("zer")], w=[pZB])
                for cix in range(2):
                    j = 2 * (t - NT_P) + cix
                    p0 = 64 * cix
                    pre_c = (f"cv{l}_{j}" in scr)
                    if pre_c:
                        pump_until((f"cv{l}_{j}", f"ck{l}_{j}"))
                    cvs = scr[f"cv{l}_{j}"] if pre_c else cv[l, j]
                    cks = scr[f"ck{l}_{j}"] if pre_c else ck[l, j]
                    cq_ = SP if pre_c else POOL
                    for q4 in range(4):
                        dma(cq_, vc[:, q4 * 4:(q4 + 1) * 4, :],
                            cvs[q4 * 512:(q4 + 1) * 512, :].rearrange("(a p) f -> p a f", p=128),
                            r=[bf(f"scr_cv{l}_{j}")] if pre_c else [], w=[bf("vc")])
                    for q4 in range(4):
                        dma(cq_, kc[:, :, :],
                            cks[q4 * 512:(q4 + 1) * 512, :].rearrange("(a p) f -> p a f", p=128),
                            r=[bf(f"scr_ck{l}_{j}")] if pre_c else [], w=[bf("kc")])
                        for c in range(4):
                            pt, ptB = nextT()
                            for a_ in range(4):
                                op(PE, mk("transpose", out=pt[:, a_ * 128:(a_ + 1) * 128],
                                          in_=kc[:, a_, c * 128:(c + 1) * 128], identity=ident[:, :]),
                                   r=[bf("kc"), bf("ident")], w=[ptB])
                            op(ACT, mk("activation", out=kcT[:, c, q4 * 512:(q4 + 1) * 512], in_=pt[:, 0:512],
                                       func=AF.Copy), r=[ptB], w=[bf("kcT")])
                    for half in range(3):
                        ps_, psB_ = nextA(exclude=(0, 1))
                        kts_ = range(half * 8, half * 8 + 8) if half < 2 else [16]
                        for a_, kt in enumerate(kts_):
                            for hd in range(8):
                                c = hd // 2
                                par = hd % 2
                                lhs = kcT[:, c, kt * 128:(kt + 1) * 128] if kt < 16 else ksT[:, c, :]
                                op(PE, mk("matmul", ps_[:, (a_ * 8 + hd) * 8:(a_ * 8 + hd) * 8 + 8], lhsT=lhs,
                                          rhs=qT[:, qs_, par, c, p0:p0 + 8], start=True, stop=True),
                                   r=[bf("kcT") if kt < 16 else bf("ksT"), qB], w=[psB_])
                        if half < 2:
                            op(ACT, mk("activation", out=Ps[:, half * 8:(half + 1) * 8, :, :],
                                       in_=ps_[:, :].rearrange("p (a b c) -> p a b c", a=8, b=8),
                                       func=AF.Exp, scale=0.125), r=[psB_], w=[bf("Ps")])
                        else:
                            op(ACT, mk("activation", out=Ps[:, 16, :, :],
                                       in_=ps_[:, 0:64].rearrange("p (b c) -> p b c", b=8),
                                       func=AF.Exp, scale=0.125), r=[psB_], w=[bf("Ps")])
                    op(DVE, mk("tensor_tensor", out=Ps[:, 0:16, :, :], in0=Ps[:, 0:16, :, :],
                               in1=smask[:, 0:16, :, :], op=ALU.mult), r=[bf("Ps"), bf("smask")], w=[bf("Ps")])
                    op(DVE, mk("tensor_tensor", out=Ps[:, 16, :, :], in0=Ps[:, 16, :, :],
                               in1=smask[:, 16 + cix, :, :], op=ALU.mult), r=[bf("Ps"), bf("smask")], w=[bf("Ps")])
                    for hd in range(8):
                        for kt in range(17):
                            rhs = vc[:, kt, hd * 64:(hd + 1) * 64] if kt < 16 else vsbf[:, hd * 64:(hd + 1) * 64]
                            op(PE, mk("matmul", pO[p0:p0 + 8, hd * 64:(hd + 1) * 64], lhsT=Ps[:, kt, hd, :],
                                      rhs=rhs, start=False, stop=(kt == 16), skip_group_check=True),
                               r=[bf("Ps"), bf("vc") if kt < 16 else bf("vsbf")], w=[pOB])
                    for hd in range(8):
                        for kt in range(17):
                            op(PE, mk("matmul", pZ[p0:p0 + 8, hd * 2:hd * 2 + 2], lhsT=Ps[:, kt, hd, :],
                                      rhs=onesc[:, 0:2], start=False, stop=(kt == 16), skip_group_check=True),
                               r=[bf("Ps"), bf("onesc")], w=[pZB])
                op(ACT, mk("activation", out=osb[:, :, 0:64], in_=pO[:, :].rearrange("p (a d) -> p a d", a=8),
                           func=AF.Copy), r=[pOB], w=[osB])
                op(ACT, mk("activation", out=osb[:, :, 64:65],
                           in_=pZ[:, 0:16].rearrange("p (a d) -> p a d", a=8)[:, :, 0:1], func=AF.Copy),
                   r=[pZB], w=[osB])
            omB = bf("omix")
            if do_attn:
                op(DVE, mk("tensor_scalar", out=rz[:, :].rearrange("p (a o) -> p a o", o=1), in0=osb[:, :, 64:65],
                           scalar1=1e-30, scalar2=None, op0=ALU.max), r=[osB], w=[bf("rz")])
                op(DVE, mk("reciprocal", out=rz[:, :], in_=rz[:, :]), r=[bf("rz")], w=[bf("rz")])
                oaB = bf("t10")
                op(DVE, mk("tensor_tensor",
                           out=t1[:, 0, 0:512].rearrange("p (a d) -> p a d", a=8), in0=osb[:, :, 0:64],
                           in1=rz[:, :].rearrange("p (a o) -> p a o", o=1).to_broadcast([128, 8, 64]), op=ALU.mult),
                   r=[osB, bf("rz")], w=[oaB])
                rmsnorm_to_bf16(t1[:, 0, 0:512], oaB, gag[:, :], bf("gag"), omix[:, 0:512], omB, 512)

            if do_hgrn:
                pA_, pAB = nextA()
                for hh in range(4):
                    for cix in range(2):
                        op(PE, (lambda hh, cix: mk("matmul",
                            pA_[cix * 64:(cix + 1) * 64, hh * 64:(hh + 1) * 64],
                            lhsT=KdT[:, hh * 128 + cix * 64: hh * 128 + (cix + 1) * 64],
                            rhs=QgT[:, hh * 128 + cix * 64: hh * 128 + (cix + 1) * 64], start=True, stop=True))(hh, cix),
                           r=[bf("KdT"), bf("QgT")], w=[pAB])
                op(DVE, mk("tensor_tensor",
                    out=AT[:, :, :], in0=pA_[:, 0:256].rearrange("p (a n) -> p a n", a=4),
                    in1=hmask[:, :].rearrange("p (o n) -> p o n", o=1).to_broadcast([128, 4, 64]), op=ALU.mult),
                   r=[pAB, bf("hmask")], w=[bf("AT")])
                pOh, pOhB = nextA()
                for cix in range(2):
                    rows = slice(cix * 64, (cix + 1) * 64)
                    if is_s:
                        j = 2 * (t - NT_P) + cix
                        dma(SP, Sst[:, :, :], sh[l, j].rearrange("h k v -> k h v"), w=[bf("Sst")])
                        op(ACT, mk("activation", out=Sbf[:, :, :], in_=Sst[:, :, :], func=AF.Copy),
                           r=[bf("Sst")], w=[bf("Sbf")])
                    pS, pSB = nextA()
                    for hh in range(4):
                        op(PE, (lambda hh, cix: mk("matmul",
                            pOh[cix * 64:(cix + 1) * 64, hh * 128:(hh + 1) * 128],
                            lhsT=QgT[:, hh * 128 + cix * 64: hh * 128 + (cix + 1) * 64],
                            rhs=Sbf[:, hh, :], start=True, stop=False))(hh, cix),
                           r=[bf("QgT"), bf("Sbf")], w=[pOhB])
                        op(PE, (lambda hh, cix: mk("matmul",
                            pOh[cix * 64:(cix + 1) * 64, hh * 128:(hh + 1) * 128],
                            lhsT=AT[cix * 64:(cix + 1) * 64, hh, :],
                            rhs=vh[cix * 64:(cix + 1) * 64, hh * 128:(hh + 1) * 128], start=False, stop=True))(hh, cix),
                           r=[bf("AT"), bf("vh")], w=[pOhB])
                        op(PE, (lambda hh, cix, pS: mk("matmul",
                            pS[:, hh * 128:(hh + 1) * 128], lhsT=Kesb[cix * 64:(cix + 1) * 64, hh * 128:(hh + 1) * 128],
                            rhs=vh[cix * 64:(cix + 1) * 64, hh * 128:(hh + 1) * 128], start=True, stop=True))(hh, cix, pS),
                           r=[bf("Kesb"), bf("vh")], w=[pSB])
                    egl_bc = eGl[:, :].rearrange("p (a c) -> p a c", c=2)[:, :, cix:cix + 1].to_broadcast([128, 4, 128])
                    op(DVE, (lambda egl_bc: mk("tensor_tensor", out=Sst[:, :, :], in0=Sst[:, :, :], in1=egl_bc,
                                                                      op=ALU.mult))(egl_bc),
                       r=[bf("Sst"), bf("eGl")], w=[bf("Sst")])
                    op(DVE, (lambda pS: mk("tensor_tensor",
                        out=Sst[:, :, :], in0=Sst[:, :, :], in1=pS[:, :].rearrange("p (a n) -> p a n", a=4),
                        op=ALU.add))(pS), r=[bf("Sst"), pSB], w=[bf("Sst")])
                    op(ACT, mk("activation", out=Sbf[:, :, :], in_=Sst[:, :, :], func=AF.Copy),
                       r=[bf("Sst")], w=[bf("Sbf")])
                    if is_s:
                        dma(SP, sso[l, j].rearrange("h k v -> k h v"), Sst[:, :, :], r=[bf("Sst")])
                if t == NT_P - 1:
                    dma(SP, spo[l].rearrange("h k v -> k h v"), Sst[:, :, :], r=[bf("Sst")])
                st, sB = next_stat()
                for hh in range(4):
                    op(ACT, (lambda hh: mk("activation",
                        out=junk[:, 0:128], in_=pOh[:, hh * 128:(hh + 1) * 128], func=AF.Square,
                        accum_out=st[:, hh:hh + 1]))(hh), r=[pOhB], w=[junkB, sB])
                rstd_chain(st[:, :], st[:, :], st[:, :], sB, 1.0 / 128, EPS)
                obB = bf("t11")
                op(DVE, mk("tensor_tensor",
                    out=t1[:, 1, 0:512].rearrange("p (a n) -> p a n", a=4),
                    in0=pOh[:, :].rearrange("p (a n) -> p a n", a=4),
                    in1=st[:, :].rearrange("p (a o) -> p a o", o=1).to_broadcast([128, 4, 128]), op=ALU.mult),
                   r=[pOhB, sB], w=[obB])
                op(POOL, mk("tensor_tensor", out=t1[:, 1, 0:512], in0=t1[:, 1, 0:512], in1=ghg[:, :], op=ALU.mult),
                   r=[obB, bf("ghg")], w=[obB])
                op(POOL, mk("tensor_tensor", out=omix[:, 512:1024], in0=t1[:, 1, 0:512], in1=sgate[:, :],
                                                   op=ALU.mult), r=[obB, bf("sgate")], w=[omB])

            transpose_to(lambda c: omix[:, c * 128:(c + 1) * 128], omB, 8, omT[:, :, :], bf("omT"))
            py = [nextA(), nextA()]
            for dh in range(2):
                for kcx in range(8):
                    op(PE, (lambda dh, kcx, p: mk("matmul",
                        p[:, :], lhsT=omT[:, kcx, :], rhs=Wout[:, kcx, dh * 512:(dh + 1) * 512],
                        start=(kcx == 0), stop=(kcx == 7)))(dh, kcx, py[dh][0]),
                       r=[bf("omT"), bf("Wout")], w=[py[dh][1]])
            st, sB = next_stat()
            for dh in range(2):
                op(ACT, (lambda dh, p: mk("activation",
                    out=junk[:, 0:512], in_=p[:, :], func=AF.Square, accum_out=st[:, dh:dh + 1]))(dh, py[dh][0]),
                   r=[py[dh][1]], w=[junkB, sB])
            op(DVE, mk("tensor_tensor", out=st[:, 2:3], in0=st[:, 0:1], in1=st[:, 1:2], op=ALU.add),
               r=[sB], w=[sB])
            rstd_chain(st[:, 2:3], st[:, 3:4], st[:, 2:3], sB, 1.0 / D, EPS)
            tB = bf("t10")
            for dh in range(2):
                op(DVE, (lambda dh, p: mk("scalar_tensor_tensor",
                    out=t1[:, 0, dh * 512:(dh + 1) * 512], in0=p[:, :], scalar=st[:, 2:3],
                    in1=gB[:, dh * 512:(dh + 1) * 512], op0=ALU.mult, op1=ALU.mult))(dh, py[dh][0]),
                   r=[py[dh][1], sB, bf("gB")], w=[tB])
            op(POOL, mk("tensor_tensor", out=xbuf[:, 0, s, :], in0=xbuf[:, 0, s, :], in1=t1[:, 0, :],
                                               op=ALU.add), r=[tB, XB], w=[XB])
            dma(SP, dst_rows(t), xbuf[:, 0, s, :], r=[XB])
            pump(2)

    if stages is None:
        plan_mix(0)
        plan_cache(0)
        plan_ffn("ff2", 0)
        plan_ffn("ff1", 1)
        plan_mix(1)
        plan_cache(1)
        plan_ffn("ff2", 1)
        ffn_stage(0, "ff1", loc_R(xin), loc_S)
        mix_stage(0, loc_S, loc_R(R))
        ffn_stage(0, "ff2", loc_R(R), loc_R(R))
        ffn_stage(1, "ff1", loc_R(R), loc_S)
        mix_stage(1, loc_S, loc_R(R))
        ffn_stage(1, "ff2", loc_R(R), loc_R(yout))
    else:
        for st_ in stages:
            if st_ == "ffn":
                ffn_stage(0, "ff1", loc_R(xin), loc_R(yout))
            elif st_ == "mix":
                mix_stage(0, loc_R(xin), loc_R(yout))
    T.barrier()
    if stages is not None:
        dma(SP, dbg[:, 0:1024], gA[:, :])
        dma(SP, dbg[:, 1024:2048], gB[:, :])
        dma(SP, dbg[:, 2048:3072], t1[:, 0, :])
        dma(SP, dbg[:, 3072:4096], t1[:, 1, :])
        dma(SP, dbg[:, 4096:4160], stat[:, :])
        dma(SP, dbg[:, 5120:6144], xbuf[:, 0, 1, :])
        dma(SP, dbg[:, 6144:7168], xbuf[:, 1, 1, :])
        T.barrier()
    T.finalize()
    build_nc.stats = {e.name: (len(e.ops), e.n_ms, e.ndma) for e in T.engs.values()}

    with ExitStack() as es:
        for e in (PE, ACT, DVE, POOL):
            e.sems = [es.enter_context(nc.semaphore(f"s_{e.name}_{i}")) for i in range(T.n_epochs(e))]
        SP.ring = [es.enter_context(nc.semaphore(f"r_sp_{i}")) for i in range(12)]
        POOL.ring = [es.enter_context(nc.semaphore(f"r_pool_{i}")) for i in range(12)]
        T.ccq.ring = [es.enter_context(nc.semaphore(f"r_cc_{i}")) for i in range(max(1, T.ccq.ndma))]
        block = es.enter_context(nc.Block())

        @block.tensor
        def _(h):
            T.replay(PE, h)

        @block.scalar
        def _(h):
            T.replay(ACT, h)

        @block.vector
        def _(h):
            T.replay(DVE, h)

        @block.gpsimd
        def _(h):
            T.replay(POOL, h)

        @block.sync
        def _(h):
            T.replay(SP, h)
    return nc


_CACHE = {}


def kernel(x_prompt, x_sample, cache_attn_k, cache_attn_v, state_hgrn, **weights):
    x_prompt = np.asarray(x_prompt, np.float32)
    x_sample = np.asarray(x_sample, np.float32)
    cache_attn_k = np.asarray(cache_attn_k, np.float32)
    cache_attn_v = np.asarray(cache_attn_v, np.float32)
    state_hgrn = np.asarray(state_hgrn, np.float32)
    if "nc" not in _CACHE:
        _CACHE["nc"] = build_nc()
        _CACHE["consts"] = make_consts()
    nc = _CACHE["nc"]
    consts = _CACHE["consts"]
    wnp = {k: np.ascontiguousarray(np.asarray(v, np.float32)) for k, v in weights.items()}
    in_maps = []
    for c in range(8):
        b, hf = c // 2, c % 2
        xs = np.zeros((2, 128, D), np.float32)
        for j in range(4):
            xs[j // 2, (j % 2) * 64:(j % 2) * 64 + 8] = x_sample[4 * c + j]
        xin = np.concatenate([x_prompt[b, hf * 2048:(hf + 1) * 2048], xs.reshape(256, D)], axis=0)
        m = {
            "xin": np.ascontiguousarray(xin),
            "ck": np.ascontiguousarray(cache_attn_k[:, 4 * c:4 * c + 4].reshape(2, 4, 2048, 512)),
            "cv": np.ascontiguousarray(cache_attn_v[:, 4 * c:4 * c + 4].reshape(2, 4, 2048, 512)),
            "sh": np.ascontiguousarray(state_hgrn[:, 4 * c:4 * c + 4]),
            "c_flag": np.full((128, 1), float(hf), np.float32),
        }
        m.update(wnp)
        m.update(consts)
        in_maps.append(m)
    res = run_bass_kernel_spmd(nc, in_maps, core_ids=list(range(8)))
    rs = res.results
    y_prompt = np.zeros((4, 4096, D), np.float32)
    y_sample = np.zeros((32, 8, D), np.float32)
    nkp = np.zeros((2, 4, 2048, 8, 64), np.float32)
    nvp = np.zeros((2, 4, 2048, 8, 64), np.float32)
    nsp = np.zeros((2, 4, 4, 128, 128), np.float32)
    nks = np.zeros((2, 32, 8, 8, 64), np.float32)
    nvs = np.zeros((2, 32, 8, 8, 64), np.float32)
    nss = np.zeros((2, 32, 4, 128, 128), np.float32)
    for b in range(4):
        for hf in range(2):
            y_prompt[b, hf * 2048:(hf + 1) * 2048] = rs[2 * b + hf]["yout"][0:2048]
        nkp[:, b] = rs[2 * b + 1]["kout"][:, 0:2048].reshape(2, 2048, 8, 64)
        nvp[:, b] = rs[2 * b + 1]["vout"][:, 0:2048].reshape(2, 2048, 8, 64)
        nsp[:, b] = rs[2 * b + 1]["spo"]
    for c in range(8):
        for j in range(4):
            r0 = (j // 2) * 128 + (j % 2) * 64
            y_sample[4 * c + j] = rs[c]["yout"][2048 + r0:2048 + r0 + 8]
            nks[:, 4 * c + j] = rs[c]["kout"][:, 2048 + r0:2048 + r0 + 8].reshape(2, 8, 8, 64)
            nvs[:, 4 * c + j] = rs[c]["vout"][:, 2048 + r0:2048 + r0 + 8].reshape(2, 8, 8, 64)
            nss[:, 4 * c + j] = rs[c]["sso"][:, j]
    return (y_prompt, y_sample, nkp, nvp, nsp, nks, nvs, nss)
```

```python
import numpy as np
import ml_dtypes
from contextlib import ExitStack
import concourse.bass as bass
import concourse.mybir as mybir
from concourse.bass_utils import run_bass_kernel_spmd

F32 = mybir.dt.float32
BF16 = mybir.dt.bfloat16
AF = mybir.ActivationFunctionType
ALU = mybir.AluOpType

D = 1024
DFF = 2816
NFC = 22
INW = 3584
NT_P = 32
NT = 34
NLOC = 18
EPS = 1e-6
EPOCH = 12000
DBG = {}
PATTERNS = ((128, 1), (512, 4), (2048, 16))


class Buf:
    __slots__ = ("lw", "rd", "name", "excl")

    def __init__(self, name="", excl=False):
        self.lw = []
        self.rd = []
        self.name = name
        self.excl = excl


class Op:
    __slots__ = ("fn", "deps", "tok", "dma", "ms", "msnum", "red")

    def __init__(self, fn, deps, tok, dma):
        self.fn = fn
        self.deps = deps
        self.tok = tok
        self.dma = dma
        self.ms = False
        self.msnum = 0
        self.red = None


class Eng:
    def __init__(self, name, is_pe=False):
        self.name = name
        self.ops = []
        self.pending = []
        self.is_pe = is_pe
        self.sems = []
        self.ring = []
        self.ndma = 0


class Tracker:
    def __init__(self):
        self.engs = {}
        self.dma_toks = []
        self.ccq = Eng("ccq")

    def eng(self, name, is_pe=False):
        e = Eng(name, is_pe)
        self.engs[name] = e
        return e

    def record(self, eng, fn, r=(), w=(), dma=False, cc=False):
        deps = set(eng.pending)
        eng.pending = []
        if any(b.excl for b in r):
            w = list(w) + [b for b in r if b.excl and b not in w]
            r = [b for b in r if not b.excl]
        is_dma = dma or cc
        for b in r:
            deps.update(b.lw)
        for b in w:
            for lw_ in b.lw:
                if not (is_dma and lw_[0] == "d" and not b.rd):
                    deps.add(lw_)
            deps.update(b.rd)
        idx = len(eng.ops)
        if cc:
            q = self.ccq
            n = q.ndma
            q.ndma += 1
            tok = ("d", n, 1, q)
            self.dma_toks.append(tok)
            dma = True
        elif dma:
            n = eng.ndma
            eng.ndma += 1
            K = 12
            slot = n % K
            val = 16 * (n // K + 1)
            if n >= K:
                deps.add(("d", slot, val - 16, eng))
            tok = ("d", slot, val, eng)
            self.dma_toks.append(tok)
        else:
            tok = ("c", eng, idx)
        eng.ops.append(Op(fn, deps, tok, dma))
        ws = set(id(b) for b in w)
        for b in w:
            if dma and not b.rd and b.lw and all(x[0] == "d" for x in b.lw):
                b.lw = b.lw + [tok]
            else:
                b.lw = [tok]
            b.rd = []
        for b in r:
            if id(b) not in ws:
                b.rd.append(tok)
        return tok

    def barrier(self):
        toks = []
        for e in self.engs.values():
            if e.ops:
                last = None
                for i in range(len(e.ops) - 1, -1, -1):
                    if not e.ops[i].dma:
                        last = i
                        break
                if last is not None:
                    toks.append(("c", e, last))
        toks.extend(self.dma_toks)
        self.dma_toks = []
        for e in self.engs.values():
            e.pending = list(set(e.pending) | set(toks))

    def finalize(self):
        for e in self.engs.values():
            known_c = {}
            known_d = {}
            for op in e.ops:
                cmax = {}
                dmax = {}
                for dep in op.deps:
                    if dep[0] == "c":
                        if dep[1] is e and e.is_pe:
                            continue
                        k = dep[1].name
                        if dep[2] > cmax.get(k, (-1, None))[0]:
                            cmax[k] = (dep[2], dep[1])
                    else:
                        k = (dep[3].name, dep[1])
                        if dep[2] > dmax.get(k, (-1, None))[0]:
                            dmax[k] = (dep[2], dep)
                red = []
                for k, (i, de) in cmax.items():
                    if known_c.get(k, -1) >= i:
                        continue
                    known_c[k] = i
                    red.append(("c", de, i))
                    de.ops[i].ms = True
                for k, (v, dep) in dmax.items():
                    if known_d.get(k, -1) >= v:
                        continue
                    known_d[k] = v
                    red.append(dep)
                op.red = red
            e.final_red = []
            cmax = {}
            dmax = {}
            for dep in e.pending:
                if dep[0] == "c":
                    if dep[1] is e:
                        continue
                    k = dep[1].name
                    if dep[2] > cmax.get(k, (-1, None))[0]:
                        cmax[k] = (dep[2], dep[1])
                else:
                    k = (dep[3].name, dep[1])
                    if dep[2] > dmax.get(k, (-1, None))[0]:
                        dmax[k] = (dep[2], dep)
            for k, (i, de) in cmax.items():
                if known_c.get(k, -1) >= i:
                    continue
                e.final_red.append(("c", de, i))
                de.ops[i].ms = True
            for k, (v, dep) in dmax.items():
                if known_d.get(k, -1) >= v:
                    continue
                e.final_red.append(dep)
        for e in self.engs.values():
            m = 0
            for op in e.ops:
                if op.ms and not op.dma:
                    m += 1
                    op.msnum = m
            e.n_ms = m

    def n_epochs(self, e):
        return max(1, (e.n_ms + EPOCH - 1) // EPOCH)

    def emit_wait(self, h, dep):
        if dep[0] == "c":
            tgt = dep[1].ops[dep[2]]
            m = tgt.msnum
            h.wait_ge(dep[1].sems[(m - 1) // EPOCH], (m - 1) % EPOCH + 1)
        else:
            h.wait_ge(dep[3].ring[dep[1]], dep[2])

    def replay(self, e, h):
        for op in e.ops:
            for dep in op.red:
                self.emit_wait(h, dep)
            ins = op.fn(h)
            if op.dma:
                if op.tok[3] is self.ccq:
                    ins.then_inc(self.ccq.ring[op.tok[1]])
                else:
                    ins.then_inc(e.ring[op.tok[1]], 16)
            elif op.ms:
                m = op.msnum
                ins.then_inc(e.sems[(m - 1) // EPOCH], 1)
        for dep in e.final_red:
            self.emit_wait(h, dep)


def _mult(diff):
    m = np.zeros(diff.shape, np.float32)
    for (w, d) in PATTERNS:
        m += ((diff >= 0) & (diff % d == 0) & (diff // d <= w // d)).astype(np.float32)
    return m


def make_consts():
    i = np.arange(128)[:, None, None]
    rel = np.arange(18)[None, :, None]
    j = np.arange(128)[None, None, :]
    amask = _mult((16 - rel) * 128 + j - i)
    ii = np.arange(128)[:, None, None, None]
    kt = np.arange(18)[None, :, None, None]
    t = np.arange(8)[None, None, None, :]
    diff = 2048 + t - (kt * 128 + ii) + np.zeros((1, 1, 8, 1), np.int64)
    sm = _mult(diff)
    s = (np.arange(128) % 64)[:, None, None]
    tn = np.arange(8)[None, None, :]
    dn = tn - s + np.zeros((1, 8, 1), np.int64)
    newm = _mult(dn) * (s < 8)
    lo = (np.arange(128) < 64)[:, None, None]
    sm[:, 16] = newm * lo
    sm[:, 17] = newm * (1 - lo)
    hm = ((np.arange(128) % 64)[:, None] <= np.arange(64)[None, :]).astype(np.float32)
    rmask = np.ones((128, 512), np.float32)
    rmask[:, ::64] = 0.0
    bf = ml_dtypes.bfloat16
    return {
        "c_amask": amask.astype(bf),
        "c_smask": sm.astype(bf),
        "c_hmask": hm.astype(bf),
        "c_rmask": rmask,
        "c_ident": np.eye(128, dtype=np.float32).astype(bf),
        "c_pmask": np.stack([(np.arange(128) < 64), (np.arange(128) >= 64)], 1).astype(np.float32),
    }


def build_nc(stages=None):
    nc = bass.Bass("TRN2", target_bir_lowering=False)

    def din(name, shape, dt=F32):
        return nc.dram_tensor(name, list(shape), dt, kind="ExternalInput").ap()

    def dout(name, shape, dt=F32):
        return nc.dram_tensor(name, list(shape), dt, kind="ExternalOutput").ap()

    xin = din("xin", [NLOC * 128, D])
    ck = din("ck", [2, 4, 2048, 512])
    cv = din("cv", [2, 4, 2048, 512])
    sh = din("sh", [2, 4, 4, 128, 128])
    Wd_ = {}
    for nm, shp in [("ff1_pre_g", [2, D]), ("ff1_w_gate", [2, D, DFF]), ("ff1_w_up", [2, D, DFF]),
                    ("ff1_w_down", [2, DFF, D]), ("ff1_post_g", [2, D]), ("mix_pre_g", [2, D]),
                    ("w_in", [2, D, INW]), ("attn_norm_g", [2, 512]), ("hgrn_lb_logits", [2, 512]),
                    ("hgrn_norm_g", [2, 128]), ("w_out", [2, D, D]), ("mix_post_g", [2, D]),
                    ("ff2_pre_g", [2, D]), ("ff2_w_gate", [2, D, DFF]), ("ff2_w_up", [2, D, DFF]),
                    ("ff2_w_down", [2, DFF, D]), ("ff2_post_g", [2, D])]:
        Wd_[nm] = din(nm, shp)
    c_amask = din("c_amask", [128, 18, 128], BF16)
    c_smask = din("c_smask", [128, 18, 8, 8], BF16)
    c_hmask = din("c_hmask", [128, 64], BF16)
    c_rmask = din("c_rmask", [128, 512], F32)
    c_ident = din("c_ident", [128, 128], BF16)
    c_pmask = din("c_pmask", [128, 2], F32)

    yout = dout("yout", [NLOC * 128, D])
    kout = dout("kout", [2, 18 * 128, 512])
    vout = dout("vout", [2, 18 * 128, 512])
    spo = dout("spo", [2, 4, 128, 128])
    sso = dout("sso", [2, 4, 4, 128, 128])
    R = nc.dram_tensor("Rscratch", [NLOC * 128, D], F32).ap()
    Ssend = [nc.dram_tensor(f"Ssend{i}", [256, D], F32).ap() for i in range(8)]
    Srecv = [nc.dram_tensor(f"Srecv{i}", [512, D], F32).ap() for i in range(8)]

    scr = {}
    conv_pending = []

    def plan_conv(key, wdr, rows, cols, colblk):
        t_ = nc.dram_tensor("scr_" + key, [rows, cols], BF16).ap()
        scr[key] = t_
        for r0 in range(0, rows, 128):
            for c0 in range(0, cols, colblk):
                conv_pending.append((key, t_[r0:r0 + 128, c0:c0 + colblk], wdr[r0:r0 + 128, c0:c0 + colblk]))

    def plan_ffn(pre, l):
        plan_conv(f"{pre}g{l}", Wd_[pre + "_w_gate"][l], D, DFF, 1408)
        plan_conv(f"{pre}u{l}", Wd_[pre + "_w_up"][l], D, DFF, 1408)
        plan_conv(f"{pre}d{l}", Wd_[pre + "_w_down"][l], DFF, D, 1024)

    def plan_mix(l):
        plan_conv(f"win{l}", Wd_["w_in"][l], D, INW, 1792)
        plan_conv(f"wout{l}", Wd_["w_out"][l], D, D, 1024)

    def plan_cache(l):
        for nm, cdr in (("cv", cv), ("ck", ck)):
            for j in range(4):
                key = f"{nm}{l}_{j}"
                t_ = nc.dram_tensor("scr_" + key, [2048, 512], BF16).ap()
                scr[key] = t_
                for q4 in range(4):
                    conv_pending.append((key, t_[q4 * 512:(q4 + 1) * 512, :], cdr[l, j, q4 * 512:(q4 + 1) * 512, :]))

    def pump(n):
        for _ in range(n):
            if not conv_pending:
                return
            key, o_, i_ = conv_pending.pop(0)
            T.record(POOL, mk("dma_start", out=o_, in_=i_), r=[], w=[bf("scr_" + key)], dma=True)

    def pump_until(keys):
        while any(k_[0] in keys for k_ in conv_pending):
            pump(1)

    def loc_R(Rt):
        return lambda i: Rt[i * 128:(i + 1) * 128, :]

    def loc_S(i):
        return Ssend[i // 2][(i % 2) * 128:(i % 2) * 128 + 128, :] if i < 16 else R[i * 128:(i + 1) * 128, :]
    c_flag = din("c_flag", [128, 1], F32)
    dbg = dout("dbg", [128, 8192]) if stages is not None else None

    off = [16512]

    def sb(name, shape, dt, at=None):
        nbytes = int(np.prod(shape[1:])) * (4 if dt == F32 else 2)
        if at is None:
            at = off[0]
            off[0] = (at + nbytes + 31) // 32 * 32
        return nc.alloc_sbuf_tensor_at(name, list(shape), dt, offset=at), at, nbytes

    ident, _, _ = sb("ident", [128, 128], BF16)
    onesc, _, _ = sb("onesc", [128, 2], BF16)
    stat, _, _ = sb("stat", [128, 64], F32)
    lbc, _, _ = sb("lbc", [128, 2, 4], F32)
    omlb, _, _ = sb("omlb", [128, 2, 4], F32)
    lgt, _, _ = sb("lgt", [128, 2, 4], F32)
    pmask, _, _ = sb("pmask", [128, 2], F32)
    zer, _, _ = sb("zer", [128, 512], BF16)
    flagc, _, _ = sb("flagc", [128, 1], F32)
    ones8, _, _ = sb("ones8", [128, 8], BF16)
    junk, _, _ = sb("junk", [128, 1024], BF16)
    gA, _, _ = sb("gA", [128, 1024], F32)
    gB, _, _ = sb("gB", [128, 1024], F32)
    xbuf, xbuf_at, _ = sb("xbuf", [128, 2, 2, 1024], F32)
    xn, _, _ = sb("xn", [128, 2, 1024], BF16)
    xnT, xnT_at, _ = sb("xnT", [128, 2, 8, 256], BF16)
    t1, t1_at, _ = sb("t1", [128, 2, 1024], F32)
    base = off[0]
    off[0] = base
    Wg, _, _ = sb("Wg", [128, 8, DFF], BF16)
    Wu, _, _ = sb("Wu", [128, 8, DFF], BF16)
    Wdn, _, _ = sb("Wdn", [128, NFC, D], BF16)
    hT, _, _ = sb("hT", [128, 2, NFC, 256], BF16)
    sg, _, _ = sb("sg", [128, 3, 256], F32)
    ffn_end = off[0]
    off[0] = base
    Win, _, _ = sb("Win", [128, 8, INW], BF16)
    Wout, _, _ = sb("Wout", [128, 8, D], BF16)
    gag, _, _ = sb("gag", [128, 512], F32)
    ghg, _, _ = sb("ghg", [128, 512], F32)
    amask, _, _ = sb("amask", [128, 18, 128], BF16)
    smask, _, _ = sb("smask", [128, 18, 8, 8], BF16)
    hmask, _, _ = sb("hmask", [128, 64], BF16)
    rmask, _, _ = sb("rmask", [128, 512], F32)
    qT, _, _ = sb("qT", [128, 2, 2, 4, 128], BF16)
    ksb, _, _ = sb("ksb", [128, 2, 512], F32, at=xbuf_at + 8192)
    vsb, _, _ = sb("vsb", [128, 2, 512], F32, at=xbuf_at + 8192 + 4096)
    qs, _, _ = sb("qs", [128, 512], F32)
    fT, _, _ = sb("fT", [128, 512], F32)
    GT, _, _ = sb("GT", [128, 512], F32)
    eG, _, _ = sb("eG", [128, 512], F32)
    eGn, _, _ = sb("eGn", [128, 512], F32)
    eGl, _, _ = sb("eGl", [128, 8], F32)
    QgT, _, _ = sb("QgT", [128, 512], BF16)
    KdT, _, _ = sb("KdT", [128, 512], BF16)
    KeT, _, _ = sb("KeT", [128, 512], BF16)
    Kesb, _, _ = sb("Kesb", [128, 512], BF16)
    vh, _, _ = sb("vh", [128, 512], BF16)
    sgate, _, _ = sb("sgate", [128, 512], F32, at=t1_at + 4096 + 2048)
    AT, _, _ = sb("AT", [128, 4, 64], BF16)
    Sst, _, _ = sb("Sst", [128, 4, 128], F32)
    Sbf, _, _ = sb("Sbf", [128, 4, 128], BF16)
    osb, _, _ = sb("osb", [128, 8, 65], F32)
    rz, _, _ = sb("rz", [128, 8], F32)
    omix, _, _ = sb("omix", [128, 1024], BF16, at=xnT_at + 4096)
    omT, _, _ = sb("omT", [128, 8, 128], BF16, at=xnT_at + 4096 + 2048)
    Pt, _, _ = sb("Pt", [128, 4, 4, 128], BF16)
    kvbase = off[0]
    kTr, _, _ = sb("kTr", [128, 18, 4, 128], BF16)
    Vp, _, _ = sb("Vp", [128, 18, 8, 66], BF16)
    kv_end = off[0]
    off[0] = kvbase
    kc, _, _ = sb("kc", [128, 4, 512], BF16)
    kcT, _, _ = sb("kcT", [128, 4, 2048], BF16)
    vc, _, _ = sb("vc", [128, 16, 512], BF16)
    Ps, _, _ = sb("Ps", [128, 17, 8, 8], BF16)
    ksT, _, _ = sb("ksT", [128, 4, 128], BF16)
    vsbf, _, _ = sb("vsbf", [128, 512], BF16)
    mix_end = max(off[0], kv_end)
    assert ffn_end <= 229376 and mix_end <= 229376, (ffn_end, mix_end)

    psA = [nc.alloc_psum_tensor(f"psA{i}", [128, 512], F32) for i in range(6)]
    psT = [nc.alloc_psum_tensor(f"psT{i}", [128, 1024], BF16) for i in range(2)]
    psAB = [Buf(f"psA{i}", excl=True) for i in range(6)]
    psTB = [Buf(f"psT{i}", excl=True) for i in range(2)]
    pa_ctr = [0]
    pt_ctr = [0]

    def nextA(exclude=()):
        while True:
            i = pa_ctr[0] % 6
            pa_ctr[0] += 1
            if i not in exclude:
                return psA[i], psAB[i]

    def nextT():
        i = pt_ctr[0] % 2
        pt_ctr[0] += 1
        return psT[i], psTB[i]

    T = Tracker()
    PE = T.eng("pe", is_pe=True)
    ACT = T.eng("act")
    DVE = T.eng("dve")
    POOL = T.eng("pool")
    SP = T.eng("sp")

    def mk(name, *a, **kw):
        return lambda h: getattr(h, name)(*a, **kw)

    def op(e, fn, r=(), w=()):
        return T.record(e, fn, r, w)

    def dma(e, out, in_, r=(), w=(), **kw):
        return T.record(e, mk("dma_start", out=out, in_=in_, **kw), r, w, dma=True)

    stat_ctr = [0]
    statB = [Buf(f"stat{i}") for i in range(16)]

    def next_stat():
        i = stat_ctr[0] % 16
        stat_ctr[0] += 1
        return stat[:, i * 4:(i + 1) * 4], statB[i]

    junkB = Buf("junk")

    def bcast_row(ap_row, n):
        return bass.AP(ap_row.tensor, ap_row.offset, [[0, 128], [1, n]])

    B = {}

    def bf(name):
        if name not in B:
            B[name] = Buf(name)
        return B[name]

    dma(SP, ident[:, :], c_ident[:, :], w=[bf("ident")])
    op(POOL, mk("memset", onesc[:, :], 1.0), w=[bf("onesc")])
    dma(SP, pmask[:, :], c_pmask[:, :], w=[bf("pmask")])
    op(POOL, mk("memset", zer[:, :], 0.0), w=[bf("zer")])
    op(POOL, mk("memset", ones8[:, :], 1.0), w=[bf("ones8")])
    dma(SP, flagc[:, :], c_flag[:, :], w=[bf("flagc")])
    op(POOL, mk("memset", lbc[:, :, :], 0.0), w=[bf("lbc")])
    for l in range(2):
        for hh in range(4):
            src = Wd_["hgrn_lb_logits"][l, hh * 128:(hh + 1) * 128].rearrange("(p o) -> p o", o=1)
            dma(SP, lgt[:, l, hh:hh + 1], src, w=[bf("lgt")])
    op(DVE, mk("tensor_tensor", out=lgt[:, 0, :], in0=lgt[:, 1, :], in1=lgt[:, 0, :], op=ALU.subtract),
       r=[bf("lgt")], w=[bf("lgt")])
    op(ACT, mk("activation", out=lbc[:, 1, :], in_=lgt[:, 0, :], func=AF.Sigmoid),
       r=[bf("lgt")], w=[bf("lbc")])
    op(DVE, mk("tensor_scalar", out=omlb[:, :, :], in0=lbc[:, :, :], scalar1=-1.0, scalar2=1.0,
                                      op0=ALU.mult, op1=ALU.add), r=[bf("lbc")], w=[bf("omlb")])

    def rstd_chain(src_ap, tmp_ap, dst_ap, sB, a, b):
        op(DVE, mk("tensor_scalar", out=tmp_ap, in0=src_ap, scalar1=a, scalar2=b,
                                          op0=ALU.mult, op1=ALU.add), r=[sB], w=[sB])
        op(ACT, mk("activation", out=tmp_ap, in_=tmp_ap, func=AF.Sqrt), r=[sB], w=[sB])
        op(DVE, mk("reciprocal", out=dst_ap, in_=tmp_ap), r=[sB], w=[sB])

    def rmsnorm_to_bf16(x_ap, xB, g_ap, gBuf, out_ap, outB, dim, scale_mul=1.0):
        st, sB = next_stat()
        op(ACT, mk("activation", out=junk[:, 0:dim], in_=x_ap, func=AF.Square, accum_out=st[:, 0:1]),
           r=[xB], w=[junkB, sB])
        rstd_chain(st[:, 0:1], st[:, 1:2], st[:, 2:3], sB, 1.0 / dim, EPS)
        op(DVE, mk("scalar_tensor_tensor", out=out_ap, in0=x_ap, scalar=st[:, 2:3], in1=g_ap,
                                                 op0=ALU.mult, op1=ALU.mult), r=[xB, sB, gBuf], w=[outB])

    def transpose_to(src_ap_fn, srcB, nchunks, dst_ap, dstB):
        pt, ptB = nextT()
        for c in range(nchunks):
            op(PE, (lambda c: mk("transpose", out=pt[:, c * 128:(c + 1) * 128], in_=src_ap_fn(c),
                                                    identity=ident[:, :]))(c),
               r=[srcB, bf("ident")], w=[ptB])
        op(ACT, mk("activation", out=dst_ap,
                                       in_=pt[:, 0:nchunks * 128].rearrange("p (c n) -> p c n", c=nchunks),
                                       func=AF.Copy), r=[ptB], w=[dstB])

    def tile_row(t):
        return slice(t * 128, (t + 1) * 128)

    def ffn_stage(l, pre, src, dst):
        T.barrier()
        wgd, wud, wdd = Wd_[pre + "_w_gate"][l], Wd_[pre + "_w_up"][l], Wd_[pre + "_w_down"][l]
        dma(SP, gA[:, :], bcast_row(Wd_[pre + "_pre_g"][l], D), w=[bf("gA")])
        dma(SP, gB[:, :], bcast_row(Wd_[pre + "_post_g"][l], D), w=[bf("gB")])
        pre_conv = (f"{pre}g{l}" in scr)
        if pre_conv:
            pump_until((f"{pre}g{l}", f"{pre}u{l}", f"{pre}d{l}"))
            wgd, wud, wdd = scr[f"{pre}g{l}"], scr[f"{pre}u{l}"], scr[f"{pre}d{l}"]
        q_ = SP if pre_conv else POOL
        for cb in range(2):
            for (wsb, wdr, nm, sk) in ((Wg, wgd, "Wg", f"scr_{pre}g{l}"), (Wu, wud, "Wu", f"scr_{pre}u{l}")):
                for kcx in range(8):
                    dma(q_, wsb[:, kcx, cb * 1408:(cb + 1) * 1408],
                        wdr[kcx * 128:(kcx + 1) * 128, cb * 1408:(cb + 1) * 1408],
                        r=[bf(sk)] if pre_conv else [], w=[bf(f"{nm}{cb}")])
        for fc in range(NFC):
            dma(q_, Wdn[:, fc, :], wdd[fc * 128:(fc + 1) * 128, :],
                r=[bf(f"scr_{pre}d{l}")] if pre_conv else [], w=[bf(f"Wd{fc // 11}")])
        ngroups = NLOC // 2

        def load_group(g):
            for ti in range(2):
                dma(SP, xbuf[:, g % 2, ti, :], src(2 * g + ti), w=[bf(f"xbuf{g % 2}")])

        load_group(0)
        for g in range(ngroups):
            s = g % 2
            XB = bf(f"xbuf{s}")
            if g + 1 < ngroups:
                load_group(g + 1)
            xnTB = bf(f"xnT{s}")
            for ti in range(2):
                xnB = bf(f"xn{ti}")
                rmsnorm_to_bf16(xbuf[:, s, ti, :], XB, gA[:, :], bf("gA"), xn[:, ti, :], xnB, D)
                transpose_to((lambda ti: lambda c: xn[:, ti, c * 128:(c + 1) * 128])(ti), xnB, 8,
                             xnT[:, s, :, ti * 128:(ti + 1) * 128], xnTB)
            hB = bf(f"hT{s}")
            for fc in range(NFC):
                wB = [bf(f"Wg{fc // 11}"), bf(f"Wu{fc // 11}")]
                pg, pgB = nextA()
                pu, puB = nextA()
                for kcx in range(8):
                    op(PE, (lambda kcx, fc, pg: mk("matmul",
                        pg[:, 0:256], lhsT=Wg[:, kcx, fc * 128:(fc + 1) * 128], rhs=xnT[:, s, kcx, :],
                        start=(kcx == 0), stop=(kcx == 7)))(kcx, fc, pg), r=[wB[0], xnTB], w=[pgB])
                for kcx in range(8):
                    op(PE, (lambda kcx, fc, pu: mk("matmul",
                        pu[:, 0:256], lhsT=Wu[:, kcx, fc * 128:(fc + 1) * 128], rhs=xnT[:, s, kcx, :],
                        start=(kcx == 0), stop=(kcx == 7)))(kcx, fc, pu), r=[wB[1], xnTB], w=[puB])
                si = fc % 3
                sgB = bf(f"sg{si}")
                op(ACT, (lambda pg, si: mk("activation", out=sg[:, si, :], in_=pg[:, 0:256], func=AF.Silu))(pg, si),
                   r=[pgB], w=[sgB])
                op(DVE, (lambda pu, si, fc: mk("tensor_tensor",
                    out=hT[:, s, fc, :], in0=sg[:, si, :], in1=pu[:, 0:256], op=ALU.mult))(pu, si, fc),
                   r=[sgB, puB], w=[hB])
            for ti in range(2):
                py = [nextA(), nextA()]
                for dh in range(2):
                    for fc in range(NFC):
                        op(PE, (lambda dh, fc, p: mk("matmul",
                            p[:, :], lhsT=hT[:, s, fc, ti * 128:(ti + 1) * 128],
                            rhs=Wdn[:, fc, dh * 512:(dh + 1) * 512],
                            start=(fc == 0), stop=(fc == NFC - 1)))(dh, fc, py[dh][0]),
                           r=[hB, bf(f"Wd{fc // 11}")], w=[py[dh][1]])
                st, sB = next_stat()
                for dh in range(2):
                    op(ACT, (lambda dh, p: mk("activation",
                        out=junk[:, 0:512], in_=p[:, :], func=AF.Square, accum_out=st[:, dh:dh + 1]))(dh, py[dh][0]),
                       r=[py[dh][1]], w=[junkB, sB])
                op(DVE, mk("tensor_tensor", out=st[:, 2:3], in0=st[:, 0:1], in1=st[:, 1:2], op=ALU.add),
                   r=[sB], w=[sB])
                rstd_chain(st[:, 2:3], st[:, 3:4], st[:, 2:3], sB, 4.0 / D, 4.0 * EPS)
                tB = bf(f"t1{ti}")
                for dh in range(2):
                    op(DVE, (lambda dh, p: mk("scalar_tensor_tensor",
                        out=t1[:, ti, dh * 512:(dh + 1) * 512], in0=p[:, :], scalar=st[:, 2:3],
                        in1=gB[:, dh * 512:(dh + 1) * 512], op0=ALU.mult, op1=ALU.mult))(dh, py[dh][0]),
                       r=[py[dh][1], sB, bf("gB")], w=[tB])
                op(POOL, mk("tensor_tensor", out=xbuf[:, s, ti, :], in0=xbuf[:, s, ti, :], in1=t1[:, ti, :],
                                                   op=ALU.add), r=[tB, XB], w=[XB])
                dma(SP, dst(2 * g + ti), xbuf[:, s, ti, :], r=[XB])
            pump(6)

    CQ, CK, CV, CBQ, CBF, CBI, CBG = 0, 512, 1024, 1536, 2048, 2560, 3072

    def mix_stage(l, src, dst):
        T.barrier()
        if not DBG.get("no_cc"):
            for i in range(8):
                T.record(POOL, mk("collective_compute", "AllGather", ALU.bypass,
                                  replica_groups=[[0, 1], [2, 3], [4, 5], [6, 7]],
                                  ins=[Ssend[i][:, :]], outs=[Srecv[i][:, :]]), r=[], w=[bf(f"Srecv{i}")], cc=True)

        def src_rows(t):
            if t < 16:
                return Srecv[t // 2][(t % 2) * 128:(t % 2) * 128 + 128, :]
            return src(t - 16)

        def src_bufs(t):
            return [bf(f"Srecv{t // 2}")] if t < 16 else []

        def dst_rows(t):
            return dst(t - 16)

        wind, woutd = Wd_["w_in"][l], Wd_["w_out"][l]
        dma(SP, gA[:, :], bcast_row(Wd_["mix_pre_g"][l], D), w=[bf("gA")])
        pump_until((f"win{l}", f"wout{l}"))
        wind, woutd = scr[f"win{l}"], scr[f"wout{l}"]
        for cb in range(2):
            for kcx in range(8):
                dma(SP, Win[:, kcx, cb * 1792:(cb + 1) * 1792],
                    wind[kcx * 128:(kcx + 1) * 128, cb * 1792:(cb + 1) * 1792], r=[bf(f"scr_win{l}")], w=[bf("Win")])
        dma(SP, gB[:, :], bcast_row(Wd_["mix_post_g"][l], D), w=[bf("gB")])
        dma(SP, gag[:, :], bcast_row(Wd_["attn_norm_g"][l], 512), w=[bf("gag")])
        for hh in range(4):
            dma(SP, ghg[:, hh * 128:(hh + 1) * 128], bcast_row(Wd_["hgrn_norm_g"][l], 128), w=[bf("ghg")])
        dma(SP, amask[:, :, :], c_amask[:, :, :], w=[bf("amask")])
        dma(SP, smask[:, :, :, :], c_smask[:, :, :, :], w=[bf("smask")])
        dma(SP, hmask[:, :], c_hmask[:, :], w=[bf("hmask")])
        dma(SP, rmask[:, :], c_rmask[:, :], w=[bf("rmask")])
        for kcx in range(8):
            dma(SP, Wout[:, kcx, :], woutd[kcx * 128:(kcx + 1) * 128, :], r=[bf(f"scr_wout{l}")], w=[bf("Wout")])
        op(POOL, mk("memset", osb[:, :, :], 1.0), w=[bf("osb")])
        op(POOL, mk("memset", Sst[:, :, :], 0.0), w=[bf("Sst")])
        op(POOL, mk("memset", Sbf[:, :, :], 0.0), w=[bf("Sbf")])

        tiles_ = [NT_P, NT_P + 1] + list(range(NT_P))
        if DBG.get("ntiles"):
            tiles_ = list(range(DBG["ntiles"]))
        if DBG.get("sample_only"):
            tiles_ = [NT_P, NT_P + 1]
        do_attn = not DBG.get("no_attn")
        do_hgrn = not DBG.get("no_hgrn")
        if not (do_attn and do_hgrn):
            op(POOL, mk("memset", omix[:, :], 0.25), w=[bf("omix")])
        if DBG.get("no_tiles"):
            tiles_ = []
        for idx_, t in enumerate(tiles_):
            is_s = t >= NT_P
            if t == 0 and idx_ > 0:
                T.barrier()
                op(POOL, mk("memset", Sst[:, :, :], 0.0), w=[bf("Sst")])
                op(POOL, mk("memset", Sbf[:, :, :], 0.0), w=[bf("Sbf")])
            s = t % 2
            XB = bf(f"xbuf{s}")
            warm = t < 16
            if idx_ == 0:
                dma(SP, xbuf[:, 0, s, :], src_rows(t), r=src_bufs(t), w=[XB])
            if idx_ + 1 < len(tiles_):
                tn_ = tiles_[idx_ + 1]
                dma(SP, xbuf[:, 0, tn_ % 2, :], src_rows(tn_), r=src_bufs(tn_), w=[bf(f"xbuf{tn_ % 2}")])
            xnB = bf("xn0")
            rmsnorm_to_bf16(xbuf[:, 0, s, :], XB, gA[:, :], bf("gA"), xn[:, 0, :], xnB, D)
            xnTB = bf("xnT0")
            transpose_to(lambda c: xn[:, 0, c * 128:(c + 1) * 128], xnB, 8, xnT[:, 0, :, 0:128], xnTB)
            xr = lambda kcx: xnT[:, 0, kcx, 0:128]
            slot = t % 18

            def aform(col0, nch):
                p, pB = nextA()
                for c in range(nch):
                    for kcx in range(8):
                        op(PE, (lambda c, kcx: mk("matmul",
                            p[:, c * 128:(c + 1) * 128], lhsT=Win[:, kcx, col0 + c * 128: col0 + (c + 1) * 128],
                            rhs=xr(kcx), start=(kcx == 0), stop=(kcx == 7)))(c, kcx),
                           r=[bf("Win"), xnTB], w=[pB])
                return p, pB

            def bform(col0):
                p, pB = nextA()
                for kcx in range(8):
                    op(PE, (lambda kcx: mk("matmul",
                        p[:, :], lhsT=xr(kcx), rhs=Win[:, kcx, col0:col0 + 512],
                        start=(kcx == 0), stop=(kcx == 7)))(kcx), r=[bf("Win"), xnTB], w=[pB])
                return p, pB

            qs_ = t % 2
            qB = bf(f"qT{qs_}")
            parts = DBG.get("parts", "qkKv")
            if not warm:
                p, pB = aform(CQ, 4)
            for par in (DBG.get("qpar", [0, 1]) if ("q" in parts and not warm) else []):
                op(DVE, (lambda p, par: mk("tensor_scalar",
                    out=qT[:, qs_, par, :, :], in0=p[:, :].rearrange("p (c n) -> p c n", c=4),
                    scalar1=pmask[:, par:par + 1], scalar2=None, op0=ALU.mult))(p, par),
                   r=[pB, bf("pmask")], w=[qB])
            p, pB = aform(CK, 4)
            if "k" not in parts:
                pass
            elif not is_s:
                kB = bf(f"kT{slot}")
                op(DVE, (lambda p: mk("tensor_copy", out=kTr[:, slot, :, :],
                                                           in_=p[:, :].rearrange("p (c n) -> p c n", c=4)))(p),
                   r=[pB], w=[kB])
            else:
                kB = bf("ksT")
                op(DVE, (lambda p: mk("tensor_copy", out=ksT[:, :, :],
                                                           in_=p[:, :].rearrange("p (c n) -> p c n", c=4)))(p),
                   r=[pB], w=[kB])
            ks_ = t % 2
            ksB = bf(f"ksb{ks_}")
            if not warm:
                p, pB = bform(CK)
            if "K" in parts and not warm:
                op(ACT, (lambda p: mk("activation", out=ksb[:, ks_, :], in_=p[:, :], func=AF.Copy))(p),
                   r=[pB], w=[ksB])
            p, pB = bform(CV)
            vsB = bf(f"vsb{ks_}")
            if "v" in parts and not warm:
                op(ACT, (lambda p: mk("activation", out=vsb[:, ks_, :], in_=p[:, :], func=AF.Copy))(p),
                   r=[pB], w=[vsB])
            if DBG.get("skip_vcopy"):
                pass
            elif warm:
                vB = bf(f"Vp{slot}")
                op(DVE, mk("tensor_scalar", out=Vp[:, slot, :, 0:64], in0=p[:, :].rearrange("p (a d) -> p a d", a=8),
                           scalar1=flagc[:, 0:1], scalar2=None, op0=ALU.mult), r=[pB, bf("flagc")], w=[vB])
                op(DVE, mk("tensor_scalar", out=Vp[:, slot, :, 64:65],
                           in0=ones8[:, :].rearrange("p (a o) -> p a o", o=1),
                           scalar1=flagc[:, 0:1], scalar2=None, op0=ALU.mult), r=[bf("ones8"), bf("flagc")], w=[vB])
            elif not is_s:
                vB = bf(f"Vp{slot}")
                op(ACT, (lambda p: mk("activation",
                    out=Vp[:, slot, :, 0:64], in_=p[:, :].rearrange("p (a d) -> p a d", a=8), func=AF.Copy))(p),
                   r=[pB], w=[vB])
                op(POOL, mk("tensor_copy", out=Vp[:, slot, :, 64:65], in_=ones8[:, :].rearrange("p (a o) -> p a o", o=1)),
                   r=[bf("ones8")], w=[vB])
            else:
                vB = bf("vsbf")
                op(DVE, (lambda p: mk("tensor_copy", out=vsbf[:, :], in_=p[:, :]))(p), r=[pB], w=[vB])
            if t >= 16:
                orow = (t - 16) * 128
                dma(SP, kout[l, orow:orow + 128, :], ksb[:, ks_, :], r=[ksB])
                dma(SP, vout[l, orow:orow + 128, :], vsb[:, ks_, :], r=[vsB])

            if DBG.get("level", 9) < 1:
                continue
            if not warm:
                p, pB = aform(CBQ, 4)
                op(ACT, (lambda p: mk("activation", out=qs[:, :], in_=p[:, :], func=AF.Silu))(p),
                   r=[pB], w=[bf("qs")])
            p, pB = aform(CBF, 4)
            op(ACT, (lambda p: mk("activation", out=fT[:, :], in_=p[:, :], func=AF.Sigmoid))(p),
               r=[pB], w=[bf("fT")])
            for hh in range(4):
                op(DVE, (lambda hh: mk("tensor_scalar",
                    out=fT[:, hh * 128:(hh + 1) * 128], in0=fT[:, hh * 128:(hh + 1) * 128],
                    scalar1=omlb[:, l, hh:hh + 1], scalar2=lbc[:, l, hh:hh + 1],
                    op0=ALU.mult, op1=ALU.add))(hh), r=[bf("fT"), bf("omlb"), bf("lbc")], w=[bf("fT")])
            op(ACT, mk("activation", out=eG[:, :], in_=fT[:, :], func=AF.Ln), r=[bf("fT")], w=[bf("eG")])
            op(DVE, mk("tensor_tensor_scan", out=GT[:, :], data0=rmask[:, :], data1=eG[:, :], initial=0.0,
                                                   op0=ALU.mult, op1=ALU.add),
               r=[bf("eG"), bf("rmask")], w=[bf("GT")])
            if not warm:
                op(ACT, mk("activation", out=eG[:, :], in_=GT[:, :], func=AF.Exp), r=[bf("GT")], w=[bf("eG")])
            op(ACT, mk("activation", out=eGn[:, :], in_=GT[:, :], func=AF.Exp, scale=-1.0),
               r=[bf("GT")], w=[bf("eGn")])
            lastc = 7 if is_s else 63
            gl_view = GT[:, :].rearrange("p (a n) -> p a n", n=64)[:, :, lastc:lastc + 1]
            op(ACT, mk("activation", out=eGl[:, :].rearrange("p (a o) -> p a o", o=1), in_=gl_view, func=AF.Exp),
               r=[bf("GT")], w=[bf("eGl")])
            if not warm:
                op(DVE, mk("tensor_tensor", out=QgT[:, :], in0=qs[:, :], in1=eG[:, :], op=ALU.mult),
                   r=[bf("qs"), bf("eG")], w=[bf("QgT")])
            op(DVE, mk("tensor_scalar", out=fT[:, :], in0=fT[:, :], scalar1=-1.0, scalar2=1.0,
                                              op0=ALU.mult, op1=ALU.add), r=[bf("fT")], w=[bf("fT")])
            op(DVE, mk("tensor_tensor", out=eGn[:, :], in0=eGn[:, :], in1=fT[:, :], op=ALU.mult),
               r=[bf("fT"), bf("eGn")], w=[bf("eGn")])
            if not warm:
                op(POOL, mk("tensor_copy", out=KdT[:, :], in_=eGn[:, :]), r=[bf("eGn")], w=[bf("KdT")])
            op(DVE, mk("tensor_tensor",
                out=KeT[:, :].rearrange("p (a n) -> p a n", n=64),
                in0=eGn[:, :].rearrange("p (a n) -> p a n", n=64),
                in1=eGl[:, :].rearrange("p (a o) -> p a o", o=1).to_broadcast([128, 8, 64]), op=ALU.mult),
               r=[bf("eGn"), bf("eGl")], w=[bf("KeT")])
            transpose_to(lambda c: KeT[:, c * 128:(c + 1) * 128], bf("KeT"), 4,
                         Kesb[:, :].rearrange("p (c n) -> p c n", c=4), bf("Kesb"))
            p, pB = bform(CBI)
            op(ACT, (lambda p: mk("activation", out=vh[:, :], in_=p[:, :], func=AF.Copy))(p),
               r=[pB], w=[bf("vh")])
            if not warm:
                p, pB = bform(CBG)
                op(ACT, (lambda p: mk("activation", out=sgate[:, :], in_=p[:, :], func=AF.Silu))(p),
                   r=[pB], w=[bf("sgate")])

            if DBG.get("level", 9) < 2:
                continue
            if warm:
                for cix in range(2):
                    pS, pSB = nextA()
                    for hh in range(4):
                        op(PE, mk("matmul", pS[:, hh * 128:(hh + 1) * 128],
                                  lhsT=Kesb[cix * 64:(cix + 1) * 64, hh * 128:(hh + 1) * 128],
                                  rhs=vh[cix * 64:(cix + 1) * 64, hh * 128:(hh + 1) * 128], start=True, stop=True),
                           r=[bf("Kesb"), bf("vh")], w=[pSB])
                    egl_bc = eGl[:, :].rearrange("p (a c) -> p a c", c=2)[:, :, cix:cix + 1].to_broadcast([128, 4, 128])
                    op(DVE, mk("tensor_tensor", out=Sst[:, :, :], in0=Sst[:, :, :], in1=egl_bc, op=ALU.mult),
                       r=[bf("Sst"), bf("eGl")], w=[bf("Sst")])
                    op(DVE, mk("tensor_tensor", out=Sst[:, :, :], in0=Sst[:, :, :],
                               in1=pS[:, :].rearrange("p (a n) -> p a n", a=4), op=ALU.add),
                       r=[bf("Sst"), pSB], w=[bf("Sst")])
                if t == 15:
                    op(DVE, mk("tensor_scalar", out=Sst[:, :, :], in0=Sst[:, :, :], scalar1=flagc[:, 0:1],
                               scalar2=None, op0=ALU.mult), r=[bf("Sst"), bf("flagc")], w=[bf("Sst")])
                    op(ACT, mk("activation", out=Sbf[:, :, :], in_=Sst[:, :, :], func=AF.Copy),
                       r=[bf("Sst")], w=[bf("Sbf")])
                pump(2)
                continue
            osB = bf("osb")
            if not do_attn:
                pass
            elif not is_s:
                kts = list(range(max(0, t - 16), t + 1))
                po = [(psA[0], psAB[0]), (psA[1], psAB[1])]
                items = [(hd, g0) for hd in range(8) for g0 in range(0, len(kts), 4)]
                st_ = {}

                def emit_S(ix):
                    hd, g0 = items[ix]
                    c = hd // 2
                    par = hd % 2
                    grp = kts[g0:g0 + 4]
                    ps_, psB_ = nextA(exclude=(0, 1))
                    st_[ix] = (ps_, psB_)
                    for gi, kt in enumerate(grp):
                        op(PE, (lambda gi, kt, ps_, c, par: mk("matmul",
                            ps_[:, gi * 128:(gi + 1) * 128], lhsT=kTr[:, kt % 18, c, :],
                            rhs=qT[:, qs_, par, c, :], start=True, stop=True))(gi, kt, ps_, c, par),
                           r=[bf(f"kT{kt % 18}"), qB], w=[psB_])

                def emit_rest(ix):
                    hd, g0 = items[ix]
                    grp = kts[g0:g0 + 4]
                    ps_, psB_ = st_.pop(ix)
                    pO, pOB = po[hd // 4]
                    pi = ix % 4
                    PB = bf(f"Pt{pi}")
                    n = len(grp)
                    rel0 = grp[0] - t + 16
                    op(ACT, (lambda n, pi, ps_: mk("activation",
                        out=Pt[:, pi, 0:n, :], in_=ps_[:, 0:n * 128].rearrange("p (a n) -> p a n", a=n),
                        func=AF.Exp, scale=0.125))(n, pi, ps_), r=[psB_], w=[PB])
                    meng = POOL if ix % 3 == 0 else DVE
                    op(meng, (lambda n, pi, rel0: mk("tensor_tensor",
                        out=Pt[:, pi, 0:n, :], in0=Pt[:, pi, 0:n, :], in1=amask[:, rel0:rel0 + n, :],
                        op=ALU.mult))(n, pi, rel0), r=[PB, bf("amask")], w=[PB])
                    for gi, kt in enumerate(grp):
                        first = (kt == kts[0])
                        last = (kt == kts[-1])
                        op(PE, (lambda gi, kt, first, last, pi, hd, pO: mk("matmul",
                            pO[:, (hd % 4) * 65:(hd % 4) * 65 + 65], lhsT=Pt[:, pi, gi, :],
                            rhs=Vp[:, kt % 18, hd, 0:65], start=first, stop=last))(gi, kt, first, last, pi, hd, pO),
                           r=[PB, bf(f"Vp{kt % 18}")], w=[pOB])

                LA = 3
                for ix in range(min(LA, len(items))):
                    emit_S(ix)
                for ix in range(len(items)):
                    if ix + LA < len(items):
                        emit_S(ix + LA)
                    emit_rest(ix)
                for hh2 in range(2):
                    op(ACT, (lambda hh2: mk("activation",
                        out=osb[:, hh2 * 4:(hh2 + 1) * 4, :],
                        in_=po[hh2][0][:, 0:260].rearrange("p (a d) -> p a d", a=4), func=AF.Copy))(hh2),
                       r=[po[hh2][1]], w=[osB])
            else:
                pO, pOB = psA[0], psAB[0]
                pZ, pZB = psA[1], psAB[1]
                op(PE, mk("matmul", pO[:, :], lhsT=zer[:, 0:128], rhs=zer[:, :], start=True, stop=True),
                   r=[bf("zer")], w=[pOB])
                op(PE, mk("matmul", pZ[:, 0:16], lhsT=zer[:, 0:128], rhs=zer[:, 0:16], start=True, stop=True),
                   r=[bf("zer")], w=[pZB])
                for cix in range(2):
                    j = 2 * (t - NT_P) + cix
                    p0 = 64 * cix
                    pre_c = (f"cv{l}_{j}" in scr)
                    if pre_c:
                        pump_until((f"cv{l}_{j}", f"ck{l}_{j}"))
                    cvs = scr[f"cv{l}_{j}"] if pre_c else cv[l, j]
                    cks = scr[f"ck{l}_{j}"] if pre_c else ck[l, j]
                    cq_ = SP if pre_c else POOL
                    for q4 in range(4):
                        dma(cq_, vc[:, q4 * 4:(q4 + 1) * 4, :],
                            cvs[q4 * 512:(q4 + 1) * 512, :].rearrange("(a p) f -> p a f", p=128),
                            r=[bf(f"scr_cv{l}_{j}")] if pre_c else [], w=[bf("vc")])
                    for q4 in range(4):
                        dma(cq_, kc[:, :, :],
                            cks[q4 * 512:(q4 + 1) * 512, :].rearrange("(a p) f -> p a f", p=128),
                            r=[bf(f"scr_ck{l}_{j}")] if pre_c else [], w=[bf("kc")])
                        for c in range(4):
                            pt, ptB = nextT()
                            for a_ in range(4):
                                op(PE, mk("transpose", out=pt[:, a_ * 128:(a_ + 1) * 128],
                                          in_=kc[:, a_, c * 128:(c + 1) * 128], identity=ident[:, :]),
                                   r=[bf("kc"), bf("ident")], w=[ptB])
                            op(ACT, mk("activation", out=kcT[:, c, q4 * 512:(q4 + 1) * 512], in_=pt[:, 0:512],
                                       func=AF.Copy), r=[ptB], w=[bf("kcT")])
                    for half in range(3):
                        ps_, psB_ = nextA(exclude=(0, 1))
                        kts_ = range(half * 8, half * 8 + 8) if half < 2 else [16]
                        for a_, kt in enumerate(kts_):
                            for hd in range(8):
                                c = hd // 2
                                par = hd % 2
                                lhs = kcT[:, c, kt * 128:(kt + 1) * 128] if kt < 16 else ksT[:, c, :]
                                op(PE, mk("matmul", ps_[:, (a_ * 8 + hd) * 8:(a_ * 8 + hd) * 8 + 8], lhsT=lhs,
                                          rhs=qT[:, qs_, par, c, p0:p0 + 8], start=True, stop=True),
                                   r=[bf("kcT") if kt < 16 else bf("ksT"), qB], w=[psB_])
                        if half < 2:
                            op(ACT, mk("activation", out=Ps[:, half * 8:(half + 1) * 8, :, :],
                                       in_=ps_[:, :].rearrange("p (a b c) -> p a b c", a=8, b=8),
                                       func=AF.Exp, scale=0.125), r=[psB_], w=[bf("Ps")])
                        else:
                            op(ACT, mk("activation", out=Ps[:, 16, :, :],
                                       in_=ps_[:, 0:64].rearrange("p (b c) -> p b c", b=8),
                                       func=AF.Exp, scale=0.125), r=[psB_], w=[bf("Ps")])
                    op(DVE, mk("tensor_tensor", out=Ps[:, 0:16, :, :], in0=Ps[:, 0:16, :, :],
                               in1=smask[:, 0:16, :, :], op=ALU.mult), r=[bf("Ps"), bf("smask")], w=[bf("Ps")])
                    op(DVE, mk("tensor_tensor", out=Ps[:, 16, :, :], in0=Ps[:, 16, :, :],
                               in1=smask[:, 16 + cix, :, :], op=ALU.mult), r=[bf("Ps"), bf("smask")], w=[bf("Ps")])
                    for hd in range(8):
                        for kt in range(17):
                            rhs = vc[:, kt, hd * 64:(hd + 1) * 64] if kt < 16 else vsbf[:, hd * 64:(hd + 1) * 64]
                            op(PE, mk("matmul", pO[p0:p0 + 8, hd * 64:(hd + 1) * 64], lhsT=Ps[:, kt, hd, :],
                                      rhs=rhs, start=False, stop=(kt == 16), skip_group_check=True),
                               r=[bf("Ps"), bf("vc") if kt < 16 else bf("vsbf")], w=[pOB])
                    for hd in range(8):
                        for kt in range(17):
                            op(PE, mk("matmul", pZ[p0:p0 + 8, hd * 2:hd * 2 + 2], lhsT=Ps[:, kt, hd, :],
                                      rhs=onesc[:, 0:2], start=False, stop=(kt == 16), skip_group_check=True),
                               r=[bf("Ps"), bf("onesc")], w=[pZB])
                op(ACT, mk("activation", out=osb[:, :, 0:64], in_=pO[:, :].rearrange("p (a d) -> p a d", a=8),
                           func=AF.Copy), r=[pOB], w=[osB])
                op(ACT, mk("activation", out=osb[:, :, 64:65],
                           in_=pZ[:, 0:16].rearrange("p (a d) -> p a d", a=8)[:, :, 0:1], func=AF.Copy),
                   r=[pZB], w=[osB])
            omB = bf("omix")
            if do_attn:
                op(DVE, mk("tensor_scalar", out=rz[:, :].rearrange("p (a o) -> p a o", o=1), in0=osb[:, :, 64:65],
                           scalar1=1e-30, scalar2=None, op0=ALU.max), r=[osB], w=[bf("rz")])
                op(DVE, mk("reciprocal", out=rz[:, :], in_=rz[:, :]), r=[bf("rz")], w=[bf("rz")])
                oaB = bf("t10")
                op(DVE, mk("tensor_tensor",
                           out=t1[:, 0, 0:512].rearrange("p (a d) -> p a d", a=8), in0=osb[:, :, 0:64],
                           in1=rz[:, :].rearrange("p (a o) -> p a o", o=1).to_broadcast([128, 8, 64]), op=ALU.mult),
                   r=[osB, bf("rz")], w=[oaB])
                rmsnorm_to_bf16(t1[:, 0, 0:512], oaB, gag[:, :], bf("gag"), omix[:, 0:512], omB, 512)

            if do_hgrn:
                pA_, pAB = nextA()
                for hh in range(4):
                    for cix in range(2):
                        op(PE, (lambda hh, cix: mk("matmul",
                            pA_[cix * 64:(cix + 1) * 64, hh * 64:(hh + 1) * 64],
                            lhsT=KdT[:, hh * 128 + cix * 64: hh * 128 + (cix + 1) * 64],
                            rhs=QgT[:, hh * 128 + cix * 64: hh * 128 + (cix + 1) * 64], start=True, stop=True))(hh, cix),
                           r=[bf("KdT"), bf("QgT")], w=[pAB])
                op(DVE, mk("tensor_tensor",
                    out=AT[:, :, :], in0=pA_[:, 0:256].rearrange("p (a n) -> p a n", a=4),
                    in1=hmask[:, :].rearrange("p (o n) -> p o n", o=1).to_broadcast([128, 4, 64]), op=ALU.mult),
                   r=[pAB, bf("hmask")], w=[bf("AT")])
                pOh, pOhB = nextA()
                for cix in range(2):
                    rows = slice(cix * 64, (cix + 1) * 64)
                    if is_s:
                        j = 2 * (t - NT_P) + cix
                        dma(SP, Sst[:, :, :], sh[l, j].rearrange("h k v -> k h v"), w=[bf("Sst")])
                        op(ACT, mk("activation", out=Sbf[:, :, :], in_=Sst[:, :, :], func=AF.Copy),
                           r=[bf("Sst")], w=[bf("Sbf")])
                    pS, pSB = nextA()
                    for hh in range(4):
                        op(PE, (lambda hh, cix: mk("matmul",
                            pOh[cix * 64:(cix + 1) * 64, hh * 128:(hh + 1) * 128],
                            lhsT=QgT[:, hh * 128 + cix * 64: hh * 128 + (cix + 1) * 64],
                            rhs=Sbf[:, hh, :], start=True, stop=False))(hh, cix),
                           r=[bf("QgT"), bf("Sbf")], w=[pOhB])
                        op(PE, (lambda hh, cix: mk("matmul",
                            pOh[cix * 64:(cix + 1) * 64, hh * 128:(hh + 1) * 128],
                            lhsT=AT[cix * 64:(cix + 1) * 64, hh, :],
                            rhs=vh[cix * 64:(cix + 1) * 64, hh * 128:(hh + 1) * 128], start=False, stop=True))(hh, cix),
                           r=[bf("AT"), bf("vh")], w=[pOhB])
                        op(PE, (lambda hh, cix, pS: mk("matmul",
                            pS[:, hh * 128:(hh + 1) * 128], lhsT=Kesb[cix * 64:(cix + 1) * 64, hh * 128:(hh + 1) * 128],
                            rhs=vh[cix * 64:(cix + 1) * 64, hh * 128:(hh + 1) * 128], start=True, stop=True))(hh, cix, pS),
                           r=[bf("Kesb"), bf("vh")], w=[pSB])
                    egl_bc = eGl[:, :].rearrange("p (a c) -> p a c", c=2)[:, :, cix:cix + 1].to_broadcast([128, 4, 128])
                    op(DVE, (lambda egl_bc: mk("tensor_tensor", out=Sst[:, :, :], in0=Sst[:, :, :], in1=egl_bc,
                                                                      op=ALU.mult))(egl_bc),
                       r=[bf("Sst"), bf("eGl")], w=[bf("Sst")])
                    op(DVE, (lambda pS: mk("tensor_tensor",
                        out=Sst[:, :, :], in0=Sst[:, :, :], in1=pS[:, :].rearrange("p (a n) -> p a n", a=4),
                        op=ALU.add))(pS), r=[bf("Sst"), pSB], w=[bf("Sst")])
                    op(ACT, mk("activation", out=Sbf[:, :, :], in_=Sst[:, :, :], func=AF.Copy),
                       r=[bf("Sst")], w=[bf("Sbf")])
                    if is_s:
                        dma(SP, sso[l, j].rearrange("h k v -> k h v"), Sst[:, :, :], r=[bf("Sst")])
                if t == NT_P - 1:
                    dma(SP, spo[l].rearrange("h k v -> k h v"), Sst[:, :, :], r=[bf("Sst")])
                st, sB = next_stat()
                for hh in range(4):
                    op(ACT, (lambda hh: mk("activation",
                        out=junk[:, 0:128], in_=pOh[:, hh * 128:(hh + 1) * 128], func=AF.Square,
                        accum_out=st[:, hh:hh + 1]))(hh), r=[pOhB], w=[junkB, sB])
                rstd_chain(st[:, :], st[:, :], st[:, :], sB, 1.0 / 128, EPS)
                obB = bf("t11")
                op(DVE, mk("tensor_tensor",
                    out=t1[:, 1, 0:512].rearrange("p (a n) -> p a n", a=4),
                    in0=pOh[:, :].rearrange("p (a n) -> p a n", a=4),
                    in1=st[:, :].rearrange("p (a o) -> p a o", o=1).to_broadcast([128, 4, 128]), op=ALU.mult),
                   r=[pOhB, sB], w=[obB])
                op(POOL, mk("tensor_tensor", out=t1[:, 1, 0:512], in0=t1[:, 1, 0:512], in1=ghg[:, :], op=ALU.mult),
                   r=[obB, bf("ghg")], w=[obB])
                op(POOL, mk("tensor_tensor", out=omix[:, 512:1024], in0=t1[:, 1, 0:512], in1=sgate[:, :],
                                                   op=ALU.mult), r=[obB, bf("sgate")], w=[omB])

            transpose_to(lambda c: omix[:, c * 128:(c + 1) * 128], omB, 8, omT[:, :, :], bf("omT"))
            py = [nextA(), nextA()]
            for dh in range(2):
                for kcx in range(8):
                    op(PE, (lambda dh, kcx, p: mk("matmul",
                        p[:, :], lhsT=omT[:, kcx, :], rhs=Wout[:, kcx, dh * 512:(dh + 1) * 512],
                        start=(kcx == 0), stop=(kcx == 7)))(dh, kcx, py[dh][0]),
                       r=[bf("omT"), bf("Wout")], w=[py[dh][1]])
            st, sB = next_stat()
            for dh in range(2):
                op(ACT, (lambda dh, p: mk("activation",
                    out=junk[:, 0:512], in_=p[:, :], func=AF.Square, accum_out=st[:, dh:dh + 1]))(dh, py[dh][0]),
                   r=[py[dh][1]], w=[junkB, sB])
            op(DVE, mk("tensor_tensor", out=st[:, 2:3], in0=st[:, 0:1], in1=st[:, 1:2], op=ALU.add),
               r=[sB], w=[sB])
            rstd_chain(st[:, 2:3], st[:, 3:4], st[:, 2:3], sB, 1.0 / D, EPS)
            tB = bf("t10")
            for dh in range(2):
                op(DVE, (lambda dh, p: mk("scalar_tensor_tensor",
                    out=t1[:, 0, dh * 512:(dh + 1) * 512], in0=p[:, :], scalar=st[:, 2:3],
                    in1=gB[:, dh * 512:(dh + 1) * 512], op0=ALU.mult, op1=ALU.mult))(dh, py[dh][0]),
                   r=[py[dh][1], sB, bf("gB")], w=[tB])
            op(POOL, mk("tensor_tensor", out=xbuf[:, 0, s, :], in0=xbuf[:, 0, s, :], in1=t1[:, 0, :],
                                               op=ALU.add), r=[tB, XB], w=[XB])
            dma(SP, dst_rows(t), xbuf[:, 0, s, :], r=[XB])
            pump(2)

    if stages is None:
        plan_mix(0)
        plan_cache(0)
        plan_ffn("ff2", 0)
        plan_ffn("ff1", 1)
        plan_mix(1)
        plan_cache(1)
        plan_ffn("ff2", 1)
        ffn_stage(0, "ff1", loc_R(xin), loc_S)
        mix_stage(0, loc_S, loc_R(R))
        ffn_stage(0, "ff2", loc_R(R), loc_R(R))
        ffn_stage(1, "ff1", loc_R(R), loc_S)
        mix_stage(1, loc_S, loc_R(R))
        ffn_stage(1, "ff2", loc_R(R), loc_R(yout))
    else:
        for st_ in stages:
            if st_ == "ffn":
                ffn_stage(0, "ff1", loc_R(xin), loc_R(yout))
            elif st_ == "mix":
                mix_stage(0, loc_R(xin), loc_R(yout))
    T.barrier()
    if stages is not None:
        dma(SP, dbg[:, 0:1024], gA[:, :])
        dma(SP, dbg[:, 1024:2048], gB[:, :])
        dma(SP, dbg[:, 2048:3072], t1[:, 0, :])
        dma(SP, dbg[:, 3072:4096], t1[:, 1, :])
        dma(SP, dbg[:, 4096:4160], stat[:, :])
        dma(SP, dbg[:, 5120:6144], xbuf[:, 0, 1, :])
        dma(SP, dbg[:, 6144:7168], xbuf[:, 1, 1, :])
        T.barrier()
    T.finalize()
    build_nc.stats = {e.name: (len(e.ops), e.n_ms, e.ndma) for e in T.engs.values()}

    with ExitStack() as es:
        for e in (PE, ACT, DVE, POOL):
            e.sems = [es.enter_context(nc.semaphore(f"s_{e.name}_{i}")) for i in range(T.n_epochs(e))]
        SP.ring = [es.enter_context(nc.semaphore(f"r_sp_{i}")) for i in range(12)]
        POOL.ring = [es.enter_context(nc.semaphore(f"r_pool_{i}")) for i in range(12)]
        T.ccq.ring = [es.enter_context(nc.semaphore(f"r_cc_{i}")) for i in range(max(1, T.ccq.ndma))]
        block = es.enter_context(nc.Block())

        @block.tensor
        def _(h):
            T.replay(PE, h)

        @block.scalar
        def _(h):
            T.replay(ACT, h)

        @block.vector
        def _(h):
            T.replay(DVE, h)

        @block.gpsimd
        def _(h):
            T.replay(POOL, h)

        @block.sync
        def _(h):
            T.replay(SP, h)
    return nc


_CACHE = {}


def kernel(x_prompt, x_sample, cache_attn_k, cache_attn_v, state_hgrn, **weights):
    x_prompt = np.asarray(x_prompt, np.float32)
    x_sample = np.asarray(x_sample, np.float32)
    cache_attn_k = np.asarray(cache_attn_k, np.float32)
    cache_attn_v = np.asarray(cache_attn_v, np.float32)
    state_hgrn = np.asarray(state_hgrn, np.float32)
    if "nc" not in _CACHE:
        _CACHE["nc"] = build_nc()
        _CACHE["consts"] = make_consts()
    nc = _CACHE["nc"]
    consts = _CACHE["consts"]
    wnp = {k: np.ascontiguousarray(np.asarray(v, np.float32)) for k, v in weights.items()}
    in_maps = []
    for c in range(8):
        b, hf = c // 2, c % 2
        xs = np.zeros((2, 128, D), np.float32)
        for j in range(4):
            xs[j // 2, (j % 2) * 64:(j % 2) * 64 + 8] = x_sample[4 * c + j]
        xin = np.concatenate([x_prompt[b, hf * 2048:(hf + 1) * 2048], xs.reshape(256, D)], axis=0)
        m = {
            "xin": np.ascontiguousarray(xin),
            "ck": np.ascontiguousarray(cache_attn_k[:, 4 * c:4 * c + 4].reshape(2, 4, 2048, 512)),
            "cv": np.ascontiguousarray(cache_attn_v[:, 4 * c:4 * c + 4].reshape(2, 4, 2048, 512)),
            "sh": np.ascontiguousarray(state_hgrn[:, 4 * c:4 * c + 4]),
            "c_flag": np.full((128, 1), float(hf), np.float32),
        }
        m.update(wnp)
        m.update(consts)
        in_maps.append(m)
    res = run_bass_kernel_spmd(nc, in_maps, core_ids=list(range(8)))
    rs = res.results
    y_prompt = np.zeros((4, 4096, D), np.float32)
    y_sample = np.zeros((32, 8, D), np.float32)
    nkp = np.zeros((2, 4, 2048, 8, 64), np.float32)
    nvp = np.zeros((2, 4, 2048, 8, 64), np.float32)
    nsp = np.zeros((2, 4, 4, 128, 128), np.float32)
    nks = np.zeros((2, 32, 8, 8, 64), np.float32)
    nvs = np.zeros((2, 32, 8, 8, 64), np.float32)
    nss = np.zeros((2, 32, 4, 128, 128), np.float32)
    for b in range(4):
        for hf in range(2):
            y_prompt[b, hf * 2048:(hf + 1) * 2048] = rs[2 * b + hf]["yout"][0:2048]
        nkp[:, b] = rs[2 * b + 1]["kout"][:, 0:2048].reshape(2, 2048, 8, 64)
        nvp[:, b] = rs[2 * b + 1]["vout"][:, 0:2048].reshape(2, 2048, 8, 64)
        nsp[:, b] = rs[2 * b + 1]["spo"]
    for c in range(8):
        for j in range(4):
            r0 = (j // 2) * 128 + (j % 2) * 64
            y_sample[4 * c + j] = rs[c]["yout"][2048 + r0:2048 + r0 + 8]
            nks[:, 4 * c + j] = rs[c]["kout"][:, 2048 + r0:2048 + r0 + 8].reshape(2, 8, 8, 64)
            nvs[:, 4 * c + j] = rs[c]["vout"][:, 2048 + r0:2048 + r0 + 8].reshape(2, 8, 8, 64)
            nss[:, 4 * c + j] = rs[c]["sso"][:, j]
    return (y_prompt, y_sample, nkp, nvp, nsp, nks, nvs, nss)
```
